# Optimizing a Trainium2 kernel written in Bass

```python
import math
import jax, jax.numpy as jnp
from jax import lax
import numpy as np

D_MODEL = 4096
BATCH = 2
SEQ = 8192
DEPTH = 1

MEM_LEN = 256
A_HEADS = 32
A_KV_HEADS = 4
A_HEAD_DIM = 64
WINDOW = 128
BLOCK = 128
N_BUCKETS = 32
MAX_DISTANCE = 128
B_HEADS = 16
B_DK = 128
B_DV = 128
CONV_W = 4
CHUNK = 64
M_HEADS = 4
M_HEAD_DIM = 128
D_FF = -(-8 * D_MODEL // (3 * 256)) * 256

A_Q = A_HEADS * A_HEAD_DIM
A_KV = A_KV_HEADS * A_HEAD_DIM
B_QK = B_HEADS * B_DK
B_V = B_HEADS * B_DV
B_CONV = 2 * B_QK + B_V
M_Q = M_HEADS * M_HEAD_DIM
IN_SIZES = (A_Q, A_KV, A_KV, B_CONV, B_V, B_HEADS, B_HEADS, M_Q, D_MODEL, D_MODEL, D_MODEL)
N_IN = sum(IN_SIZES)
EPS = 1e-6
NEG = -1e30

kernel_name = 'hybrid_swa_sink_gdn_memory_block'


def rms_norm(x, g):
    xf = x.astype(jnp.float32)
    y = xf * lax.rsqrt(jnp.mean(xf * xf, axis=-1, keepdims=True) + EPS)
    return (y * g.astype(jnp.float32)).astype(x.dtype)


def l2_norm(x):
    return x * lax.rsqrt(jnp.sum(x * x, axis=-1, keepdims=True) + EPS)


def split_cols(t, sizes):
    out = []
    off = 0
    for n in sizes:
        out.append(t[..., off:off + n])
        off += n
    return out


def t5_bucket(n):
    max_exact = N_BUCKETS // 2
    nf = jnp.maximum(n, 1).astype(jnp.float32)
    large = max_exact + (jnp.log(nf / max_exact) / math.log(MAX_DISTANCE / max_exact)
                         * (N_BUCKETS - max_exact)).astype(jnp.int32)
    large = jnp.minimum(large, N_BUCKETS - 1)
    return jnp.where(n < max_exact, n, large)


def swa_sink_attention(q, k, v, sink, rel_bias):
    bsz, s_len = q.shape[0], q.shape[1]
    nb = s_len // BLOCK
    grp = A_HEADS // A_KV_HEADS
    qb = q.reshape(bsz, nb, BLOCK, A_KV_HEADS, grp, A_HEAD_DIM)
    kb = k.reshape(bsz, nb, BLOCK, A_KV_HEADS, A_HEAD_DIM)
    vb = v.reshape(bsz, nb, BLOCK, A_KV_HEADS, A_HEAD_DIM)

    def with_prev(t):
        prev = jnp.pad(t[:, :-1], ((0, 0), (1, 0), (0, 0), (0, 0), (0, 0)))
        return jnp.concatenate([prev, t], axis=2)

    kw, vw = with_prev(kb), with_prev(vb)
    s = jnp.einsum('bnqhgd,bnkhd->bnhgqk', qb, kw).astype(jnp.float32) * (A_HEAD_DIM ** -0.5)
    qi = jnp.arange(BLOCK)[:, None]
    kj = jnp.arange(2 * BLOCK)[None, :]
    dist = qi + BLOCK - kj
    band = (dist >= 0) & (dist < WINDOW)
    exists = (jnp.arange(nb)[:, None, None] > 0) | (kj >= BLOCK)[None]
    mask = band[None] & exists
    bias = rel_bias[t5_bucket(jnp.maximum(dist, 0))].astype(jnp.float32)
    bias = bias.transpose(2, 0, 1).reshape(A_KV_HEADS, grp, BLOCK, 2 * BLOCK)
    s = jnp.where(mask[None, :, None, None], s + bias, NEG)
    sk = sink.astype(jnp.float32).reshape(A_KV_HEADS, grp)[:, :, None, None]
    m = jnp.maximum(jnp.max(s, axis=-1, keepdims=True), sk)
    p = jnp.exp(s - m)
    p = p / (jnp.sum(p, axis=-1, keepdims=True) + jnp.exp(sk - m))
    o = jnp.einsum('bnhgqk,bnkhd->bnqhgd', p.astype(v.dtype), vw)
    return o.reshape(bsz, s_len, A_Q)


def causal_conv(x, w):
    s_len = x.shape[1]
    xp = jnp.pad(x, ((0, 0), (CONV_W - 1, 0), (0, 0)))
    y = xp[:, 0:s_len] * w[0]
    for i in range(1, CONV_W):
        y = y + xp[:, i:i + s_len] * w[i]
    return y


def gated_delta_rule(q, k, v, g, beta):
    bsz, s_len, nh, dk = q.shape
    dv = v.shape[-1]
    n = s_len // CHUNK

    def chunk(t):
        return t.reshape(bsz, n, CHUNK, nh, -1).transpose(0, 3, 1, 2, 4)

    q, k, v = chunk(q), chunk(k), chunk(v)
    g = g.reshape(bsz, n, CHUNK, nh).transpose(0, 3, 1, 2)
    beta = beta.reshape(bsz, n, CHUNK, nh).transpose(0, 3, 1, 2)
    gc = jnp.cumsum(g, axis=-1)
    tri = jnp.tril(jnp.ones((CHUNK, CHUNK), dtype=bool))
    strict = jnp.tril(jnp.ones((CHUNK, CHUNK), dtype=bool), -1)
    decay = jnp.exp(jnp.where(tri, gc[..., :, None] - gc[..., None, :], -jnp.inf))
    kb = k * beta[..., None]
    a_mat = jnp.where(strict, jnp.einsum('bhncd,bhnsd->bhncs', kb, k) * decay, 0.0)
    rhs = jnp.concatenate([v * beta[..., None], kb * jnp.exp(gc)[..., None]], axis=-1)
    sol = lax.linalg.triangular_solve(a_mat + jnp.eye(CHUNK, dtype=a_mat.dtype), rhs,
                                      left_side=True, lower=True)
    u, w = sol[..., :dv], sol[..., dv:]
    att = jnp.where(tri, jnp.einsum('bhncd,bhnsd->bhncs', q, k) * decay, 0.0)

    def step(state, inp):
        q_c, k_c, u_c, w_c, att_c, gc_c = inp
        v_new = u_c - jnp.einsum('bhck,bhkv->bhcv', w_c, state)
        o_c = (jnp.einsum('bhck,bhkv->bhcv', q_c * jnp.exp(gc_c)[..., None], state)
               + jnp.einsum('bhcs,bhsv->bhcv', att_c, v_new))
        g_last = gc_c[..., -1:]
        state = (state * jnp.exp(g_last)[..., None]
                 + jnp.einsum('bhck,bhcv->bhkv', k_c * jnp.exp(g_last - gc_c)[..., None], v_new))
        return state, o_c

    xs = tuple(jnp.moveaxis(t, 2, 0) for t in (q, k, u, w, att, gc))
    state0 = jnp.zeros((bsz, nh, dk, dv), jnp.float32)
    _, o = lax.scan(step, state0, xs)
    return o.transpose(1, 0, 3, 2, 4).reshape(bsz, s_len, nh, dv)


def setup_inputs(seed: int = 0) -> dict:
    key = jax.random.key(seed)
    ks = jax.random.split(key, 24)
    f32 = jnp.float32

    def nrm(k, shape, fan_in):
        return jax.random.normal(k, shape, f32) * (fan_in ** -0.5)

    def gain(k, shape):
        return 1.0 + 0.02 * jax.random.normal(k, shape, f32)

    dt = jnp.exp(jax.random.uniform(ks[5], (DEPTH, B_HEADS), f32, math.log(1e-3), math.log(0.1)))
    return {
        'x': jax.random.normal(ks[0], (BATCH, SEQ, D_MODEL), f32),
        'mem': jax.random.normal(ks[1], (BATCH, MEM_LEN, D_MODEL), f32),
        'rel_bias': 0.1 * jax.random.normal(ks[2], (N_BUCKETS, A_HEADS), f32),
        'g_mix': gain(ks[3], (DEPTH, D_MODEL)),
        'w_in': nrm(ks[4], (DEPTH, D_MODEL, N_IN), D_MODEL),
        'conv_w': nrm(ks[6], (DEPTH, CONV_W, B_CONV), CONV_W),
        'a_log': jnp.log(jax.random.uniform(ks[7], (DEPTH, B_HEADS), f32, 1.0, 16.0)),
        'dt_bias': dt + jnp.log(-jnp.expm1(-dt)),
        'g_dn_out': gain(ks[8], (DEPTH, B_DV)),
        'sinks': 0.5 * jax.random.normal(ks[9], (DEPTH, A_HEADS), f32),
        'g_mem': gain(ks[10], (DEPTH, D_MODEL)),
        'w_mem_kv': nrm(ks[11], (DEPTH, D_MODEL, 2 * M_Q), D_MODEL),
        'w_br_a': nrm(ks[12], (DEPTH, A_Q, D_MODEL), A_Q),
        'w_br_b': nrm(ks[13], (DEPTH, B_V, D_MODEL), B_V),
        'w_br_m': nrm(ks[14], (DEPTH, M_Q, D_MODEL), M_Q),
        'w_o': nrm(ks[15], (DEPTH, D_MODEL, D_MODEL), D_MODEL),
        'g_ffn': gain(ks[16], (DEPTH, D_MODEL)),
        'w_ffn_in': nrm(ks[17], (DEPTH, D_MODEL, 2 * D_FF), D_MODEL),
        'w_ffn_out': nrm(ks[18], (DEPTH, D_FF, D_MODEL), D_FF),
        'g_final': gain(ks[19], (D_MODEL,)),
    }


def reference(x, mem, rel_bias, g_mix, w_in, conv_w, a_log, dt_bias, g_dn_out, sinks, g_mem,
              w_mem_kv, w_br_a, w_br_b, w_br_m, w_o, g_ffn, w_ffn_in, w_ffn_out, g_final):
    f32 = jnp.float32
    bsz, s_len, _ = x.shape
    for l in range(DEPTH):
        h = rms_norm(x, g_mix[l])
        proj = h @ w_in[l]
        a_q, a_k, a_v, b_qkv, b_z, b_b, b_a, m_q, gt_a, gt_b, gt_m = split_cols(proj, IN_SIZES)

        o_a = swa_sink_attention(a_q.reshape(bsz, s_len, A_HEADS, A_HEAD_DIM),
                                 a_k.reshape(bsz, s_len, A_KV_HEADS, A_HEAD_DIM),
                                 a_v.reshape(bsz, s_len, A_KV_HEADS, A_HEAD_DIM),
                                 sinks[l], rel_bias)

        qkv = jax.nn.silu(causal_conv(b_qkv, conv_w[l])).astype(f32)
        bq, bk, bv = split_cols(qkv, (B_QK, B_QK, B_V))
        bq = l2_norm(bq.reshape(bsz, s_len, B_HEADS, B_DK)) * (B_DK ** -0.5)
        bk = l2_norm(bk.reshape(bsz, s_len, B_HEADS, B_DK))
        bv = bv.reshape(bsz, s_len, B_HEADS, B_DV)
        beta = jax.nn.sigmoid(b_b.astype(f32))
        g = -jnp.exp(a_log[l].astype(f32)) * jax.nn.softplus(b_a.astype(f32) + dt_bias[l].astype(f32))
        o_b = gated_delta_rule(bq, bk, bv, g, beta)
        o_b = rms_norm(o_b, g_dn_out[l]) * jax.nn.silu(b_z.reshape(bsz, s_len, B_HEADS, B_DV).astype(f32))
        o_b = o_b.reshape(bsz, s_len, B_V).astype(x.dtype)

        mk, mv = split_cols(rms_norm(mem, g_mem[l]) @ w_mem_kv[l], (M_Q, M_Q))
        mk = mk.reshape(bsz, MEM_LEN, M_HEADS, M_HEAD_DIM)
        mv = mv.reshape(bsz, MEM_LEN, M_HEADS, M_HEAD_DIM)
        sm = jnp.einsum('bshd,bmhd->bhsm', m_q.reshape(bsz, s_len, M_HEADS, M_HEAD_DIM), mk)
        pm = jax.nn.softmax(sm.astype(f32) * (M_HEAD_DIM ** -0.5), axis=-1).astype(x.dtype)
        o_m = jnp.einsum('bhsm,bmhd->bshd', pm, mv).reshape(bsz, s_len, M_Q)

        y = (jax.nn.sigmoid(gt_a) * (o_a @ w_br_a[l])
             + jax.nn.sigmoid(gt_b) * (o_b @ w_br_b[l])
             + jax.nn.sigmoid(gt_m) * (o_m @ w_br_m[l]))
        x = x + y @ w_o[l]

        gate, up = split_cols(rms_norm(x, g_ffn[l]) @ w_ffn_in[l], (D_FF, D_FF))
        x = x + (jax.nn.silu(gate) * up) @ w_ffn_out[l]
    return rms_norm(x, g_final)
```

```python
import math
from contextlib import ExitStack

import numpy as np
import ml_dtypes

import concourse.bass as bass
import concourse.mybir as mybir
from concourse.bass_utils import run_bass_kernel_spmd

F32 = mybir.dt.float32
BF16 = mybir.dt.bfloat16
I32 = mybir.dt.int32
AF = mybir.ActivationFunctionType
ALU = mybir.AluOpType

D = 4096
KC = D // 128
D_FF = 11008
FC = D_FF // 128
N_IN = 23584
GATE0 = N_IN - 3 * D
NSLOT = 4
EPS = 1e-6
NEGM = -30000.0
NC1 = 24 * 128


class Buf:
    __slots__ = ("w", "r", "excl")

    def __init__(self):
        self.w = {}
        self.r = {}
        self.excl = False


def _merge(d, tok):
    s, v = tok
    k = id(s)
    if k not in d or d[k][1] < v:
        d[k] = (s, v)


class Slot:
    def __init__(self, sem):
        self.sem = sem
        self.count = 0


class TT:
    def __init__(self, t):
        self.t = t
        self.b = Buf()
        self.slot = None


class Prog:
    ENG = ("pe", "act", "dve", "pool", "sp")

    def __init__(self, nc, es):
        self.nc = nc
        self.es = es
        self.q = {k: [] for k in self.ENG}
        self.sem = {k: es.enter_context(nc.semaphore("s_" + k)) for k in self.ENG}
        self.cnt = {k: 0 for k in self.ENG}
        self.waited = {k: {} for k in self.ENG}
        self.nslots = 0
        self.ninst = 0
        self.nt = 0

    def slot(self):
        self.nslots += 1
        sl = Slot(self.es.enter_context(self.nc.semaphore("d%d" % self.nslots)))
        if not hasattr(self, "slot_of"):
            self.slot_of = {}
        self.slot_of[id(sl.sem)] = sl
        return sl

    def sb(self, es, shape, dt, slot=False):
        self.nt += 1
        t = TT(es.enter_context(self.nc.sbuf_tensor("t%d" % self.nt, shape, dt)))
        if slot:
            t.slot = self.slot()
        return t

    def ps(self, es, shape, dt):
        self.nt += 1
        t = TT(es.enter_context(self.nc.psum_tensor("p%d" % self.nt, shape, dt)))
        t.b.excl = True
        return t

    def _deps(self, eng, reads, writes):
        need = {}
        for b in reads:
            for tok in b.w.values():
                _merge(need, tok)
            if b.excl:
                for tok in b.r.values():
                    _merge(need, tok)
        for b in writes:
            for tok in b.w.values():
                _merge(need, tok)
            for tok in b.r.values():
                _merge(need, tok)
        own = id(self.sem[eng])
        w = self.waited[eng]
        out = []
        slot_of = getattr(self, "slot_of", {})
        for k, (s, v) in need.items():
            if eng == "pe" and k == own:
                continue
            if k in slot_of:
                v = slot_of[k].count
            if w.get(k, 0) >= v:
                continue
            w[k] = v
            out.append((s, v))
        return out

    def _mark(self, tok, reads, writes):
        for b in reads:
            _merge(b.r, tok)
        for b in writes:
            _merge(b.w, tok)

    def op(self, eng, fn, reads=(), writes=()):
        waits = self._deps(eng, reads, writes)
        self.cnt[eng] += 1
        s = self.sem[eng]
        tok = (s, self.cnt[eng])
        self.q[eng].append((fn, waits, s, 1))
        self.ninst += 1 + len(waits)
        self._mark(tok, reads, writes)
        return tok

    def dma(self, eng, out, in_, slot, reads=(), writes=()):
        waits = self._deps(eng, reads, writes)
        slot.count += 16
        tok = (slot.sem, slot.count)
        self.q[eng].append((lambda e: e.dma_start(out=out, in_=in_), waits, slot.sem, 16))
        self.ninst += 1 + len(waits)
        self._mark(tok, reads, writes)
        return tok

    def custom(self, eng, fn, slot, inc, reads=(), writes=()):
        waits = self._deps(eng, reads, writes)
        slot.count += inc
        tok = (slot.sem, slot.count)
        self.q[eng].append((fn, waits, slot.sem, inc))
        self._mark(tok, reads, writes)
        return tok

    def flush(self, final_waits=()):
        nc = self.nc
        q = self.q

        def run(e, items):
            for fn, waits, s, n in items:
                for (ws, wv) in waits:
                    e.wait_ge(ws, wv)
                ins = fn(e)
                if s is not None:
                    ins.then_inc(s, n)

        with nc.Block() as block:
            @block.tensor
            def _(e):
                run(e, q["pe"])

            @block.scalar
            def _(e):
                run(e, q["act"])

            @block.vector
            def _(e):
                run(e, q["dve"])

            @block.gpsimd
            def _(e):
                run(e, q["pool"])

            @block.sync
            def _(e):
                run(e, q["sp"])
                for (ws, wv) in final_waits:
                    e.wait_ge(ws, wv)

        self.q = {k: [] for k in self.ENG}

    def mm(self, ob, out, lhsT, rhs, rd, start=True, stop=True):
        return self.op("pe", lambda e: e.matmul(out, lhsT=lhsT, rhs=rhs, start=start, stop=stop), rd, [ob])

    def tr(self, ob, out, in_, ident, rd):
        return self.op("pe", lambda e: e.transpose(out, in_, ident), rd, [ob])

    def act(self, ob, out, in_, func, rd, bias=None, scale=None, accum=None, eng="act"):
        kw = {}
        if bias is not None:
            kw["bias"] = bias
        if scale is not None:
            kw["scale"] = scale
        if accum is not None:
            kw["accum_out"] = accum
        wr = ob if isinstance(ob, (list, tuple)) else [ob]
        return self.op("act", lambda e: e.activation(out=out, in_=in_, func=func, **kw), rd, wr)

    def copy(self, eng, ob, out, in_, rd):
        if eng == "act":
            return self.op("act", lambda e: e.activation(out=out, in_=in_, func=AF.Copy), rd, [ob])
        return self.op(eng, lambda e: e.tensor_copy(out=out, in_=in_), rd, [ob])

    def tt(self, ob, out, in0, in1, op, rd, eng="dve"):
        return self.op(eng, lambda e: e.tensor_tensor(out=out, in0=in0, in1=in1, op=op), rd, [ob])

    def ts(self, ob, out, in0, s1, op0, rd, s2=None, op1=None, eng="dve"):
        if op1 is None:
            return self.op(eng, lambda e: e.tensor_scalar(out=out, in0=in0, scalar1=s1, scalar2=None, op0=op0), rd, [ob])
        return self.op(eng, lambda e: e.tensor_scalar(out=out, in0=in0, scalar1=s1, scalar2=s2, op0=op0, op1=op1), rd, [ob])

    def stt(self, ob, out, in0, scalar, in1, op0, op1, rd, eng="dve"):
        return self.op(eng, lambda e: e.scalar_tensor_tensor(out=out, in0=in0, scalar=scalar, in1=in1, op0=op0, op1=op1), rd, [ob])

    def recip(self, ob, out, in_, rd):
        return self.op("dve", lambda e: e.reciprocal(out=out, in_=in_), rd, [ob])

    def memset(self, eng, ob, ap, val):
        return self.op(eng, lambda e: e.memset(ap, val), [], [ob])


def build(S, dbg=False):
    OWN = S // 4
    T1 = 512
    NT1 = S // T1
    T2 = min(512, OWN)
    NT2 = OWN // T2
    G2 = T2 // 128

    nc = bass.Bass("TRN2", target_bir_lowering=False)

    def din(name, shape, dt=F32):
        return nc.dram_tensor(name, list(shape), dt, kind="ExternalInput").ap()

    x_full = din("x_full", [S, D])
    x_own = din("x_own", [OWN, D])
    seg = din("seg", [1, 1], I32)
    mem = din("mem", [256, D])
    w1 = din("w1", [D, NC1])
    w_mkv = din("w_mkv", [D, 256])
    if not dbg:
        w_gate = din("w_gate", [D, 3 * D])
        w_br_a = din("w_br_a", [2048, D])
        w_br_b = din("w_br_b", [2048, D])
        w_br_m = din("w_br_m", [512, D])
        w_o = din("w_o", [D, D])
        w_fi = din("w_fi", [D, 2 * D_FF])
        w_fo = din("w_fo", [D_FF, D])
    else:
        dbg_out = nc.dram_tensor("dbg", [(S // min(256, S // 4)) * 4 * 1152, min(256, S // 4)], BF16, kind="ExternalOutput").ap()
    gcol_in = din("gcol", [128, 4, KC])
    gfin_in = din("gfin", [128, D])
    biasT_in = din("biasT", [128, 2 * 8 * 128])
    sinkrep_in = din("sinkrep", [64, 1024])
    convw_in = din("convw", [128, 48])
    gdn_in = din("gdn", [128, 1])
    alog_in = din("alogrep", [128, 16])
    dtb_in = din("dtbrep", [128, 16])
    cst_in = din("cst", [6, 128, 128])
    out = nc.dram_tensor("out", [OWN, D], F32, kind="ExternalOutput").ap()

    CH = min(256, OWN)
    NCHK = S // CH
    ocat = nc.dram_tensor("ocat", [NCHK * 1152, CH], BF16)
    ogath = nc.dram_tensor("ogath", [NCHK * 4 * 1152, CH], BF16)
    x1scr = nc.dram_tensor("x1scr", [OWN, D], F32)

    with ExitStack() as es:
        P = Prog(nc, es)
        ident_f = P.sb(es, [128, 128], F32)
        ident_b = P.sb(es, [128, 128], BF16)
        U_f = P.sb(es, [128, 128], F32)
        ones_f = P.sb(es, [128, 128], F32)
        ones_b = P.sb(es, [128, 128], BF16)
        maskneg = P.sb(es, [128, 128], F32)
        mstrict = P.sb(es, [128, 128], F32)
        gcol = P.sb(es, [128, 4, KC], F32, slot=True)
        hb = P.sb(es, [128, D], BF16)
        ss = P.sb(es, [128, 2], F32)
        xs = [P.sb(es, [128, D], F32, slot=True) for _ in range(1)]
        hT = P.sb(es, [128, KC, 512], BF16)
        wsl = [P.sb(es, [128, 8, 512], BF16, slot=True) for _ in range(NSLOT)]
        psf = [P.ps(es, [128, 512], F32) for _ in range(6)]
        psb = [P.ps(es, [128, 8, 128], BF16) for _ in range(2)]
        cslot = P.slot()
        st = {"wi": 0, "pf": 0, "pb": 0, "xi": 0}

        def next_w():
            i = st["wi"] % NSLOT
            st["wi"] += 1
            return wsl[i]

        def next_pf():
            i = st["pf"] % 6
            st["pf"] += 1
            return psf[i]

        def next_pb():
            i = st["pb"] % 2
            st["pb"] += 1
            return psb[i]

        cbufs = [ident_f, U_f, ones_f, maskneg, mstrict]
        for i, tt_ in enumerate(cbufs):
            P.dma("sp", tt_.t[:, :], cst_in[i, :, :], cslot, [], [tt_.b])
        cslot2 = P.slot()
        P.dma("pool", ident_b.t[:, :], cst_in[0, :, :], cslot2, [], [ident_b.b])
        P.dma("pool", ones_b.t[:, :], cst_in[2, :, :], cslot2, [], [ones_b.b])
        ctok = (cslot.sem, cslot.count)
        for tt_ in cbufs:
            tt_.b.w = {id(cslot.sem): ctok}
        ctok2 = (cslot2.sem, cslot2.count)
        for tt_ in [ident_b, ones_b]:
            tt_.b.w = {id(cslot2.sem): ctok2}

        P.dma("sp", gcol.t[:, :, :], gcol_in, gcol.slot, [], [gcol.b])
        st["gi"] = 0

        def load_gfull(i):
            st["gi"] = i

        def norm_group(src_ap, src_bufs):
            xt = xs[0]
            st["xi"] += 1
            P.dma("sp", xt.t[:, :], src_ap, xt.slot, src_bufs, [xt.b])
            P.op("act", lambda e: e.memzero(ss.t[:, 0:1]), [], [ss.b])
            P.act([hb.b, ss.b], hb.t[:, :], xt.t[:, :], AF.Square, [xt.b], accum=ss.t[:, 0:1])
            P.act(ss.b, ss.t[:, 1:2], ss.t[:, 0:1], AF.Sqrt, [ss.b], bias=EPS, scale=1.0 / D)
            P.recip(ss.b, ss.t[:, 1:2], ss.t[:, 1:2], [ss.b])
            return xt

        def norm_transpose(src_ap, src_bufs, g, dst, dstb):
            xt = norm_group(src_ap, src_bufs)
            P.act(hb.b, hb.t[:, :], xt.t[:, :], AF.Copy, [xt.b, ss.b], scale=ss.t[:, 1:2])
            gi = st["gi"]
            for q4 in range(4):
                pb = next_pb()
                for k8 in range(8):
                    k = q4 * 8 + k8
                    P.tr(pb.b, pb.t[:, k8, :], hb.t[:, k * 128:(k + 1) * 128], ident_b.t[:, :], [hb.b, ident_b.b])
                for k8 in range(8):
                    k = q4 * 8 + k8
                    if k8 % 2 == 0:
                        P.act(dstb, dst[:, k, g * 128:(g + 1) * 128], pb.t[:, k8, :], AF.Copy, [pb.b, gcol.b],
                              scale=gcol.t[:, gi, k:k + 1])
                    else:
                        P.ts(dstb, dst[:, k, g * 128:(g + 1) * 128], pb.t[:, k8, :], gcol.t[:, gi, k:k + 1], ALU.mult,
                             [pb.b, gcol.b])

        def stream_B(wsrc, col0, ncols, kchunks, rhs_fn, rhs_bufs, ntok, epilogue):
            nblk = (ncols + 127) // 128
            banks = [next_pf() for _ in range(nblk)]
            nk = len(kchunks)
            for s0 in range(0, nk, 8):
                sub = kchunks[s0:s0 + 8]
                wt = next_w()
                runs = []
                for i, (wk, _) in enumerate(sub):
                    if runs and runs[-1][1] + runs[-1][2] == wk:
                        runs[-1][2] += 1
                    else:
                        runs.append([i, wk, 1])
                for (i0, wk0, n) in runs:
                    P.dma("pool", wt.t[:, i0:i0 + n, 0:ncols], wsrc[:, wk0:wk0 + n, col0:col0 + ncols], wt.slot, [], [wt.b])
                for cb in range(nblk):
                    cw = min(128, ncols - cb * 128)
                    for i, (wk, ri) in enumerate(sub):
                        kk = s0 + i
                        P.mm(banks[cb].b, banks[cb].t[0:cw, 0:ntok], wt.t[:, i, cb * 128:cb * 128 + cw], rhs_fn(ri),
                             [wt.b] + rhs_bufs, start=(kk == 0), stop=(kk == nk - 1))
            for cb in range(nblk):
                epilogue(cb, banks[cb])

        def stream_A(wsrc, col0, kchunks, lhs_fn, lhs_bufs, ngrp, epilogue):
            banks = [next_pf() for _ in range(ngrp)]
            nk = len(kchunks)
            for s0 in range(0, nk, 8):
                sub = kchunks[s0:s0 + 8]
                wt = next_w()
                wk0 = sub[0][0]
                n = len(sub)
                P.dma("pool", wt.t[:, 0:n, :], wsrc[:, wk0:wk0 + n, col0:col0 + 512], wt.slot, [], [wt.b])
                for tg in range(ngrp):
                    for i, (wk, ri) in enumerate(sub):
                        kk = s0 + i
                        P.mm(banks[tg].b, banks[tg].t[:, :], lhs_fn(ri, tg), wt.t[:, i, :],
                             [wt.b] + lhs_bufs, start=(kk == 0), stop=(kk == nk - 1))
            for tg in range(ngrp):
                epilogue(tg, banks[tg])

        KALL = [(k, k) for k in range(KC)]

        with ExitStack() as e1:
            mkT = P.sb(e1, [128, 256], BF16)
            mvt = P.sb(e1, [128, 2, 128], BF16)
            biasT = P.sb(e1, [128, 2, 1024], BF16, slot=True)
            sinkexp = P.sb(e1, [64, 1024], F32, slot=True)
            convw = P.sb(e1, [128, 48], F32, slot=True)
            gdn = P.sb(e1, [128, 1], F32, slot=True)
            nea = P.sb(e1, [128, 16], F32, slot=True)
            dtb = P.sb(e1, [128, 16], F32, slot=True)
            qT = P.sb(e1, [128, 4, 512], BF16)
            kbuf = P.sb(e1, [128, 640], BF16)
            mqT = P.sb(e1, [128, 512], BF16)
            vT = P.sb(e1, [64, 512], BF16)
            ba_sb = P.sb(e1, [128, 512], F32)
            vtok = P.sb(e1, [128, 5, 64], BF16)
            ba_tok = P.sb(e1, [128, 4, 8], F32)
            beta_t = P.sb(e1, [128, 4, 4], F32)
            g_t = P.sb(e1, [128, 4, 4], F32)
            sp1 = P.sb(e1, [128, 4, 4], F32)
            sp2 = P.sb(e1, [128, 4, 4], F32)
            gcg = P.sb(e1, [128, 32], F32)
            egc = P.sb(e1, [128, 16], F32)
            bege = P.sb(e1, [128, 16], F32)
            ek2 = P.sb(e1, [128, 16], F32)
            egt = P.sb(e1, [128, 16], F32)
            raw = [P.sb(e1, [128, 515], F32) for _ in range(4)]
            hist = P.sb(e1, [128, 12, 3], F32)
            cacc = [P.sb(e1, [128, 512], F32) for _ in range(1)]
            slb = [P.sb(e1, [128, 512], F32) for _ in range(1)]
            sqb = [P.sb(e1, [128, 512], BF16) for _ in range(1)]
            rnb = [P.sb(e1, [128, 512], F32) for _ in range(1)]
            qh = [P.sb(e1, [128, 512], BF16) for _ in range(4)]
            kh = [P.sb(e1, [128, 512], BF16) for _ in range(4)]
            vg = [P.sb(e1, [128, 512], BF16) for _ in range(4)]
            zT = [P.sb(e1, [128, 512], BF16) for _ in range(4)]
            oraw = [P.sb(e1, [128, 512], F32) for _ in range(4)]
            obTs = [P.sb(e1, [128, 512], BF16, slot=True) for _ in range(2)]
            oaTs = [P.sb(e1, [64, 8, 128], BF16, slot=True) for _ in range(2)]
            omT = P.sb(e1, [128, 512], BF16, slot=True)
            Sf = [P.sb(e1, [128, 128], F32) for _ in range(4)]
            Sb = [P.sb(e1, [128, 128], BF16) for _ in range(4)]
            pT = [P.sb(e1, [128, 1024], BF16) for _ in range(2)]
            pmT = P.sb(e1, [128, 2, 512], BF16)
            den = P.sb(e1, [128, 512], F32)
            ocat_b = [Buf() for _ in range(NCHK)]
            ogath_b = [Buf() for _ in range(NCHK)]
            ccs = P.slot()

            def oc_dst(row0, nrows, tokabs):
                ci = tokabs // CH
                return ci, ocv[ci * 1152 + row0:ci * 1152 + row0 + nrows, :]

            def gather_chunk(ci):
                P.custom("pool", lambda e, ci=ci: e.collective_compute(
                    "AllGather", ALU.bypass, replica_groups=[[0, 1, 2, 3], [4, 5, 6, 7]],
                    ins=[ocat.ap()[ci * 1152:(ci + 1) * 1152, :].opt()],
                    outs=[ogath.ap()[ci * 4608:(ci + 1) * 4608, :].opt()]),
                    ccs, 1, [ocat_b[ci]], [ogath_b[ci]])
            print("sbuf left before ctx", nc.sbuf_bytes_remaining, flush=True)

            class Ctx:
                pass

            NCH = 4
            ctxs = []
            for i in range(NCH):
                c = Ctx()
                c.gB = P.sb(e1, [128, 128], F32)
                c.X = P.sb(e1, [128, 128], F32)
                c.Dm = c.X
                c.Dst = P.sb(e1, [128, 128], F32)
                c.Egc = P.sb(e1, [128, 128], BF16)
                c.qt = P.sb(e1, [128, 128], BF16)
                c.L = [P.sb(e1, [128, 256], F32) for _ in range(2)]
                c.IB = P.sb(e1, [128, 128], F32)
                c.att = P.sb(e1, [128, 128], BF16)
                c.attT = P.sb(e1, [128, 128], BF16)
                c.r = [P.sb(e1, [128, 256], F32) for _ in range(2)]
                c.k2 = P.sb(e1, [128, 128], BF16)
                c.wT = P.sb(e1, [128, 128], BF16)
                c.vn = P.sb(e1, [128, 128], BF16)
                ctxs.append(c)

            P.dma("pool", biasT.t[:, :, :], biasT_in.rearrange("p (k n) -> p k n", k=2), biasT.slot, [], [biasT.b])
            P.dma("sp", sinkexp.t[:, :], sinkrep_in, sinkexp.slot, [], [sinkexp.b])
            P.act(sinkexp.b, sinkexp.t[:, :], sinkexp.t[:, :], AF.Exp, [sinkexp.b])
            P.dma("sp", convw.t[:, :], convw_in, convw.slot, [], [convw.b])
            P.dma("sp", gdn.t[:, :], gdn_in, gdn.slot, [], [gdn.b])
            P.dma("sp", nea.t[:, :], alog_in, nea.slot, [], [nea.b])
            P.dma("sp", dtb.t[:, :], dtb_in, dtb.slot, [], [dtb.b])
            P.act(nea.b, nea.t[:, :], nea.t[:, :], AF.Exp, [nea.b])
            P.ts(nea.b, nea.t[:, :], nea.t[:, :], -1.0, ALU.mult, [nea.b])
            P.memset("dve", hist.b, hist.t[:, :, :], 0.0)
            for hd in range(4):
                P.memset("dve", Sf[hd].b, Sf[hd].t[:, :], 0.0)
                P.memset("dve", Sb[hd].b, Sb[hd].t[:, :], 0.0)

            def finish():
                dslot = P.slot()
                P.dma("pool", dbg_out[0:NCHK * 1152, :], ocat.ap(), dslot, ocat_b, [])
                P.flush(final_waits=[(dslot.sem, dslot.count)])
                return nc

            if dbg == "cp1":
                return finish()
            load_gfull(1)
            for g in range(2):
                norm_transpose(mem[g * 128:(g + 1) * 128, :], [], g, hT.t, hT.b)
            wmv = w_mkv.rearrange("(kc p) n -> p kc n", p=128)

            def ep_mk(cb, bank):
                P.copy("act", mkT.b, mkT.t[:, :], bank.t[:, 0:256], [bank.b])

            stream_B(wmv, 0, 128, KALL, lambda ri: hT.t[:, ri, 0:256], [hT.b], 256, ep_mk)
            mvb = [next_pf() for _ in range(2)]
            for s0 in range(0, KC, 8):
                wt = next_w()
                P.dma("pool", wt.t[:, 0:8, 0:128], wmv[:, s0:s0 + 8, 128:256], wt.slot, [], [wt.b])
                for mg in range(2):
                    for i in range(8):
                        k = s0 + i
                        P.mm(mvb[mg].b, mvb[mg].t[:, 0:128], hT.t[:, k, mg * 128:(mg + 1) * 128], wt.t[:, i, 0:128],
                             [wt.b, hT.b], start=(k == 0), stop=(k == KC - 1))
            for mg in range(2):
                P.copy("act", mvt.b, mvt.t[:, mg, :], mvb[mg].t[:, 0:128], [mvb[mg].b])

            if dbg == "cp2":
                return finish()
            load_gfull(0)
            w1v = w1.rearrange("(kc p) n -> p kc n", p=128)
            ocv = ocat.ap()

            for t in range(NT1):
                tok0 = t * T1
                if t == 0:
                    for g in range(4):
                        norm_transpose(x_full[tok0 + g * 128: tok0 + (g + 1) * 128, :], [], g, hT.t, hT.b)

                if dbg == "cp3":
                    return finish()
                rhs_h = lambda ri: hT.t[:, ri, :]

                def ep_q(cb, bank):
                    P.act(qT.b, qT.t[:, cb, :], bank.t[:, :], AF.Copy, [bank.b], scale=0.125)

                stream_B(w1v, 0, 512, KALL, rhs_h, [hT.b], 512, ep_q)

                def ep_g1(cb, bank):
                    if cb == 0:
                        P.copy("act", kbuf.b, kbuf.t[:, 128:640], bank.t[:, :], [bank.b])
                    elif cb == 1:
                        P.act(mqT.b, mqT.t[:, :], bank.t[:, :], AF.Copy, [bank.b], scale=128.0 ** -0.5)
                    else:
                        P.copy("dve", vT.b, vT.t[:, :], bank.t[0:64, :], [bank.b])
                        P.copy("dve", ba_sb.b, ba_sb.t[64:72, :], bank.t[64:72, :], [bank.b])

                stream_B(w1v, 512, 328, KALL, rhs_h, [hT.b], 512, ep_g1)

                pb = next_pb()
                for j in range(4):
                    P.tr(pb.b, pb.t[:, j, 0:64], vT.t[0:64, j * 128:(j + 1) * 128], ident_b.t[0:64, 0:64], [vT.b, ident_b.b])
                P.copy("act", vtok.b, vtok.t[:, 1:5, :], pb.t[:, 0:4, 0:64], [pb.b])
                pf = next_pf()
                for j in range(4):
                    P.tr(pf.b, pf.t[:, j * 8:(j + 1) * 8], ba_sb.t[64:72, j * 128:(j + 1) * 128], ident_f.t[64:72, 64:72],
                         [ba_sb.b, ident_f.b])
                P.copy("dve", ba_tok.b, ba_tok.t[:, :, :], pf.t[:, 0:32].rearrange("p (j c) -> p j c", j=4), [pf.b])
                P.act(beta_t.b, beta_t.t[:, :, :], ba_tok.t[:, :, 0:4], AF.Sigmoid, [ba_tok.b])
                dtb3 = dtb.t[:, :].rearrange("p (j c) -> p j c", j=4)
                nea3 = nea.t[:, :].rearrange("p (j c) -> p j c", j=4)
                P.tt(sp1.b, sp1.t[:, :, :], ba_tok.t[:, :, 4:8], dtb3, ALU.add, [ba_tok.b, dtb.b])
                P.ts(sp2.b, sp2.t[:, :, :], sp1.t[:, :, :], -1.0, ALU.mult, [sp1.b])
                P.tt(sp2.b, sp2.t[:, :, :], sp2.t[:, :, :], sp1.t[:, :, :], ALU.max, [sp1.b, sp2.b])
                P.act(sp2.b, sp2.t[:, :, :], sp2.t[:, :, :], AF.Exp, [sp2.b], scale=-1.0)
                P.act(sp2.b, sp2.t[:, :, :], sp2.t[:, :, :], AF.Ln, [sp2.b], bias=1.0)
                P.stt(sp1.b, sp1.t[:, :, :], sp1.t[:, :, :], 0.0, sp2.t[:, :, :], ALU.max, ALU.add, [sp1.b, sp2.b])
                P.tt(g_t.b, g_t.t[:, :, :], sp1.t[:, :, :], nea3, ALU.mult, [sp1.b, nea.b])
                pf = next_pf()
                g16 = g_t.t[:, :, :].rearrange("p j c -> p (j c)")
                P.mm(pf.b, pf.t[:, 0:16], U_f.t[:, :], g16, [U_f.b, g_t.b])
                P.mm(pf.b, pf.t[:, 16:32], ones_f.t[:, :], g16, [ones_f.b, g_t.b])
                P.copy("dve", gcg.b, gcg.t[:, :], pf.t[:, 0:32], [pf.b])
                P.act(egc.b, egc.t[:, :], gcg.t[:, 0:16], AF.Exp, [gcg.b])
                P.act(egt.b, egt.t[:, :], gcg.t[:, 16:32], AF.Exp, [gcg.b])
                P.tt(ek2.b, ek2.t[:, :], gcg.t[:, 16:32], gcg.t[:, 0:16], ALU.subtract, [gcg.b])
                P.act(ek2.b, ek2.t[:, :], ek2.t[:, :], AF.Exp, [ek2.b])
                P.tt(bege.b, bege.t[:, :], beta_t.t[:, :, :].rearrange("p j c -> p (j c)"), egc.t[:, :], ALU.mult, [beta_t.b, egc.b])

                def ep_raw(cb, bank):
                    r = raw[cb]
                    P.copy("act" if cb % 2 == 0 else "dve", r.b, r.t[:, 3:515], bank.t[:, :], [bank.b])

                def conv_group(kind):
                    for hd in range(4):
                        blk = kind * 4 + hd
                        r = raw[hd]
                        ca = cacc[0]
                        P.copy("dve", r.b, r.t[:, 0:3], hist.t[:, blk, :], [hist.b])
                        P.ts(ca.b, ca.t[:, :], r.t[:, 0:512], convw.t[:, blk * 4:blk * 4 + 1], ALU.mult, [r.b, convw.b])
                        for i in range(1, 4):
                            P.stt(ca.b, ca.t[:, :], r.t[:, i:i + 512], convw.t[:, blk * 4 + i:blk * 4 + i + 1], ca.t[:, :],
                                  ALU.mult, ALU.add, [r.b, convw.b, ca.b])
                        P.copy("dve", hist.b, hist.t[:, blk, :], r.t[:, 512:515], [r.b])
                        if kind == 2:
                            P.act(vg[hd].b, vg[hd].t[:, :], ca.t[:, :], AF.Silu, [ca.b])
                            continue
                        sl = slb[0]
                        sq = sqb[0]
                        rn = rnb[0]
                        P.act(sl.b, sl.t[:, :], ca.t[:, :], AF.Silu, [ca.b])
                        P.act(sq.b, sq.t[:, :], sl.t[:, :], AF.Square, [sl.b])
                        pf = next_pf()
                        P.mm(pf.b, pf.t[:, :], ones_b.t[:, :], sq.t[:, :], [ones_b.b, sq.b])
                        P.act(rn.b, rn.t[:, :], pf.t[:, :], AF.Sqrt, [pf.b], bias=EPS)
                        P.recip(rn.b, rn.t[:, :], rn.t[:, :], [rn.b])
                        if kind == 0:
                            P.stt(qh[hd].b, qh[hd].t[:, :], sl.t[:, :], 128.0 ** -0.5, rn.t[:, :], ALU.mult, ALU.mult, [sl.b, rn.b])
                        else:
                            P.tt(kh[hd].b, kh[hd].t[:, :], sl.t[:, :], rn.t[:, :], ALU.mult, [sl.b, rn.b])

                for gi in range(3):
                    stream_B(w1v, 1024 + gi * 512, 512, KALL, rhs_h, [hT.b], 512, ep_raw)
                    conv_group(gi)

                def ep_z(cb, bank):
                    P.act(zT[cb].b, zT[cb].t[:, :], bank.t[:, :], AF.Silu, [bank.b])

                stream_B(w1v, 2560, 512, KALL, rhs_h, [hT.b], 512, ep_z)

                if dbg == "cp4":
                    return finish()
                for j in range(4):
                    nblk = t * 4 + j
                    kbs = [1] if nblk == 0 else [0, 1]
                    sT = {}
                    for kb in kbs:
                        kcol = (j + kb) * 128
                        banks = [next_pf(), next_pf()]
                        for half in range(2):
                            bk = banks[half]
                            for h4 in range(4):
                                h = half * 4 + h4
                                p0 = (h % 2) * 64
                                P.mm(bk.b, bk.t[:, h4 * 128:(h4 + 1) * 128], ident_b.t[:, :],
                                     biasT.t[:, kb, h * 128:(h + 1) * 128], [ident_b.b, biasT.b], start=True, stop=False)
                                P.mm(bk.b, bk.t[:, h4 * 128:(h4 + 1) * 128], kbuf.t[p0:p0 + 64, kcol:kcol + 128],
                                     qT.t[p0:p0 + 64, h // 2, j * 128:(j + 1) * 128], [kbuf.b, qT.b],
                                     start=False, stop=True)
                            P.act(pT[kb].b, pT[kb].t[:, half * 512:(half + 1) * 512], bk.t[:, :], AF.Exp, [bk.b])
                    if dbg == "cp4a":
                        return finish()
                    for half in range(2):
                        bo = next_pf()
                        bd = next_pf()
                        for i, kb in enumerate(kbs):
                            P.mm(bo.b, bo.t[0:64, :], vtok.t[:, j + kb, :], pT[kb].t[:, half * 512:(half + 1) * 512],
                                 [vtok.b, pT[kb].b], start=(i == 0), stop=(i == len(kbs) - 1))
                        for i, kb in enumerate(kbs):
                            P.mm(bd.b, bd.t[0:64, :], ones_b.t[:, 0:64], pT[kb].t[:, half * 512:(half + 1) * 512],
                                 [ones_b.b, pT[kb].b], start=(i == 0), stop=(i == len(kbs) - 1))
                        P.tt(den.b, den.t[0:64, :], bd.t[0:64, :], sinkexp.t[:, half * 512:(half + 1) * 512], ALU.add,
                             [bd.b, sinkexp.b])
                        P.recip(den.b, den.t[0:64, :], den.t[0:64, :], [den.b])
                        oaT = oaTs[j % 2]
                        P.tt(oaT.b, oaT.t[:, half * 4:(half + 1) * 4, :],
                             bo.t[0:64, :].rearrange("p (h q) -> p h q", h=4),
                             den.t[0:64, :].rearrange("p (h q) -> p h q", h=4), ALU.mult, [bo.b, den.b])
                    oaT = oaTs[j % 2]
                    if dbg == "cp4b":
                        return finish()
                    ci, dst = oc_dst(0, 512, tok0 + j * 128)
                    c0_ = (tok0 + j * 128) % CH
                    P.dma("sp", dst[:, c0_:c0_ + 128].rearrange("(h d) t -> d h t", h=8),
                          oaT.t[:, :, :], oaT.slot, [oaT.b], [ocat_b[ci]])
                    if dbg == "cp4c":
                        return finish()
                P.copy("dve", kbuf.b, kbuf.t[:, 0:128], kbuf.t[:, 512:640], [kbuf.b])
                P.copy("dve", vtok.b, vtok.t[:, 0, :], vtok.t[:, 4, :], [vtok.b])

                if dbg == "cp5":
                    return finish()
                for mg in range(2):
                    pf = next_pf()
                    P.mm(pf.b, pf.t[:, :], mkT.t[:, mg * 128:(mg + 1) * 128], mqT.t[:, :], [mkT.b, mqT.b])
                    P.act(pmT.b, pmT.t[:, mg, :], pf.t[:, :], AF.Exp, [pf.b])
                bo = next_pf()
                bd = next_pf()
                for mg in range(2):
                    P.mm(bo.b, bo.t[:, :], mvt.t[:, mg, :], pmT.t[:, mg, :], [mvt.b, pmT.b], start=(mg == 0), stop=(mg == 1))
                for mg in range(2):
                    P.mm(bd.b, bd.t[:, :], ones_b.t[:, :], pmT.t[:, mg, :], [ones_b.b, pmT.b], start=(mg == 0), stop=(mg == 1))
                P.recip(den.b, den.t[:, :], bd.t[:, :], [bd.b])
                P.tt(omT.b, omT.t[:, :], bo.t[:, :], den.t[:, :], ALU.mult, [bo.b, den.b])
                for part in range(512 // CH):
                    ci, dst = oc_dst(1024, 128, tok0 + part * CH)
                    P.dma("sp", dst, omT.t[:, part * CH:(part + 1) * CH], omT.slot, [omT.b], [ocat_b[ci]])

                if dbg == "cp6":
                    return finish()
                for pair in range(4):
                    if t + 1 < NT1 and not dbg:
                        nt0 = tok0 + T1 + pair * 128
                        norm_transpose(x_full[nt0:nt0 + 128, :], [], pair, hT.t, hT.b)
                    chains = [(pair, hd) for hd in range(4)]
                    cx = {ch: ctxs[i] for i, ch in enumerate(chains)}
                    col = {ch: ch[0] * 4 + ch[1] for ch in chains}
                    tsl = {ch: slice(ch[0] * 128, (ch[0] + 1) * 128) for ch in chains}
                    pA = {}
                    for ch in chains:
                        c = cx[ch]
                        cc = col[ch]
                        P.ts(c.gB.b, c.gB.t[:, :], ones_f.t[:, :], g16[:, cc:cc + 1], ALU.mult, [ones_f.b, g_t.b])
                    for ch in chains:
                        c = cx[ch]
                        pA[ch] = next_pf()
                        P.mm(pA[ch].b, pA[ch].t[:, 0:128], c.gB.t[:, :], U_f.t[:, :], [c.gB.b, U_f.b])
                    for ch in chains:
                        c = cx[ch]
                        cc = col[ch]
                        P.tt(c.X.b, c.X.t[:, :], maskneg.t[:, :], pA[ch].t[:, 0:128], ALU.subtract, [maskneg.b, pA[ch].b])
                        P.act(c.Egc.b, c.Egc.t[:, :], pA[ch].t[:, 0:128], AF.Exp, [pA[ch].b])
                        P.act(c.Dm.b, c.Dm.t[:, :], c.X.t[:, :], AF.Exp, [c.X.b, gcg.b], bias=gcg.t[:, cc:cc + 1])
                        P.tt(c.Dst.b, c.Dst.t[:, :], c.Dm.t[:, :], mstrict.t[:, :], ALU.mult, [c.Dm.b, mstrict.b])
                        P.tt(c.qt.b, c.qt.t[:, :], qh[ch[1]].t[:, tsl[ch]], c.Egc.t[:, :], ALU.mult, [qh[ch[1]].b, c.Egc.b])
                    if dbg == "g1":
                        return finish()
                    pB = {}
                    for ch in chains:
                        hd = ch[1]
                        pB[ch] = next_pf()
                        P.mm(pB[ch].b, pB[ch].t[:, 0:128], kh[hd].t[:, tsl[ch]], kh[hd].t[:, tsl[ch]], [kh[hd].b])
                        P.mm(pB[ch].b, pB[ch].t[:, 128:256], qh[hd].t[:, tsl[ch]], kh[hd].t[:, tsl[ch]], [kh[hd].b, qh[hd].b])
                    for ch in chains:
                        c = cx[ch]
                        cc = col[ch]
                        bcol = beta_t.t[:, :, :].rearrange("p j c -> p (j c)")[:, cc:cc + 1]
                        P.stt(c.L[0].b, c.L[0].t[:, 0:128], pB[ch].t[:, 0:128], bcol, c.Dst.t[:, :], ALU.mult, ALU.mult,
                              [pB[ch].b, beta_t.b, c.Dst.b])
                        P.tt(c.att.b, c.att.t[:, :], pB[ch].t[:, 128:256], c.Dm.t[:, :], ALU.mult, [pB[ch].b, c.Dm.b])
                    if dbg == "g2":
                        return finish()
                    pTr = {}
                    for ch in chains:
                        c = cx[ch]
                        hd = ch[1]
                        pTr[ch] = next_pb()
                        pb = pTr[ch]
                        pfa = next_pf()
                        P.tr(pfa.b, pfa.t[:, 0:128], c.L[0].t[:, 0:128], ident_f.t[:, :], [c.L[0].b, ident_f.b])
                        P.tr(pb.b, pb.t[:, 1, :], c.att.t[:, :], ident_b.t[:, :], [c.att.b, ident_b.b])
                        P.tr(pb.b, pb.t[:, 2, :], kh[hd].t[:, tsl[ch]], ident_b.t[:, :], [kh[hd].b, ident_b.b])
                        P.tr(pb.b, pb.t[:, 3, :], vg[hd].t[:, tsl[ch]], ident_b.t[:, :], [vg[hd].b, ident_b.b])
                        cc = col[ch]
                        bcol = beta_t.t[:, :, :].rearrange("p j c -> p (j c)")[:, cc:cc + 1]
                        P.copy("act", c.L[0].b, c.L[0].t[:, 128:256], pfa.t[:, 0:128], [pfa.b])
                        P.tt(c.IB.b, c.IB.t[:, :], ident_f.t[:, :], pfa.t[:, 0:128], ALU.subtract, [ident_f.b, pfa.b])
                        P.copy("act", c.attT.b, c.attT.t[:, :], pb.t[:, 1, :], [pb.b])
                        P.act(c.r[0].b, c.r[0].t[:, 0:128], pb.t[:, 3, :], AF.Copy, [pb.b, beta_t.b], scale=bcol)
                        P.act(c.r[0].b, c.r[0].t[:, 128:256], pb.t[:, 2, :], AF.Copy, [pb.b, bege.b], scale=bege.t[:, cc:cc + 1])
                        P.ts(c.k2.b, c.k2.t[:, :], pb.t[:, 2, :], ek2.t[:, cc:cc + 1], ALU.mult, [pb.b, ek2.b])
                    if dbg == "g3":
                        return finish()
                    cur = 0
                    for lv in range(7):
                        if lv > 0:
                            pL = {}
                            for ch in chains:
                                c = cx[ch]
                                Lp = c.L[(lv - 1) % 2]
                                pL[ch] = next_pf()
                                P.mm(pL[ch].b, pL[ch].t[:, 0:128], Lp.t[:, 128:256], Lp.t[:, 0:128], [Lp.b])
                                P.mm(pL[ch].b, pL[ch].t[:, 128:256], Lp.t[:, 0:128], Lp.t[:, 128:256], [Lp.b])
                            for ch in chains:
                                c = cx[ch]
                                Ln_ = c.L[lv % 2]
                                if lv < 6:
                                    P.copy("act", Ln_.b, Ln_.t[:, :], pL[ch].t[:, 0:256], [pL[ch].b])
                                P.tt(c.IB.b, c.IB.t[:, :], ident_f.t[:, :], pL[ch].t[:, 128:256], ALU.add, [ident_f.b, pL[ch].b])
                        pR = {}
                        for ch in chains:
                            c = cx[ch]
                            pR[ch] = next_pf()
                            P.mm(pR[ch].b, pR[ch].t[:, 0:256], c.IB.t[:, :], c.r[cur].t[:, :], [c.IB.b, c.r[cur].b])
                        for i, ch in enumerate(chains):
                            c = cx[ch]
                            P.copy("act" if i % 2 == 0 else "dve", c.r[1 - cur].b, c.r[1 - cur].t[:, :], pR[ch].t[:, 0:256], [pR[ch].b])
                        cur = 1 - cur
                    if dbg == "g4":
                        return finish()
                    for ch in chains:
                        c = cx[ch]
                        pfw = next_pf()
                        P.tr(pfw.b, pfw.t[:, 0:128], c.r[cur].t[:, 128:256], ident_f.t[:, :], [c.r[cur].b, ident_f.b])
                        P.copy("act", c.wT.b, c.wT.t[:, :], pfw.t[:, 0:128], [pfw.b])
                    if dbg == "g5":
                        return finish()
                    for jj in range(1):
                        j = pair
                        chs = [(j, hd) for hd in range(4)]
                        pW = {}
                        for ch in chs:
                            c = cx[ch]
                            hd = ch[1]
                            pW[ch] = next_pf()
                            P.mm(pW[ch].b, pW[ch].t[:, 0:128], c.wT.t[:, :], Sb[hd].t[:, :], [c.wT.b, Sb[hd].b])
                        for ch in chs:
                            c = cx[ch]
                            P.tt(c.vn.b, c.vn.t[:, :], c.r[cur].t[:, 0:128], pW[ch].t[:, 0:128], ALU.subtract, [c.r[cur].b, pW[ch].b])
                        pO = {}
                        for ch in chs:
                            c = cx[ch]
                            hd = ch[1]
                            pO[ch] = next_pf()
                            P.mm(pO[ch].b, pO[ch].t[:, 0:128], Sb[hd].t[:, :], c.qt.t[:, :], [Sb[hd].b, c.qt.b], start=True, stop=False)
                            P.mm(pO[ch].b, pO[ch].t[:, 0:128], c.vn.t[:, :], c.attT.t[:, :], [c.vn.b, c.attT.b], start=False, stop=True)
                            P.mm(pO[ch].b, pO[ch].t[:, 128:256], c.k2.t[:, :], c.vn.t[:, :], [c.k2.b, c.vn.b])
                        for ch in chs:
                            c = cx[ch]
                            hd = ch[1]
                            cc = col[ch]
                            P.copy("act", oraw[hd].b, oraw[hd].t[:, tsl[ch]], pO[ch].t[:, 0:128], [pO[ch].b])
                            P.stt(Sf[hd].b, Sf[hd].t[:, :], Sf[hd].t[:, :], egt.t[:, cc:cc + 1], pO[ch].t[:, 128:256],
                                  ALU.mult, ALU.add, [Sf[hd].b, egt.b, pO[ch].b])
                            P.copy("act", Sb[hd].b, Sb[hd].t[:, :], Sf[hd].t[:, :], [Sf[hd].b])

                if dbg == "g6":
                    return finish()
                for hd in range(4):
                    sq = sqb[0]
                    rn = rnb[0]
                    P.act(sq.b, sq.t[:, :], oraw[hd].t[:, :], AF.Square, [oraw[hd].b])
                    pf = next_pf()
                    P.mm(pf.b, pf.t[:, :], ones_b.t[:, :], sq.t[:, :], [ones_b.b, sq.b])
                    P.act(rn.b, rn.t[:, :], pf.t[:, :], AF.Sqrt, [pf.b], bias=EPS, scale=1.0 / 128)
                    P.recip(rn.b, rn.t[:, :], rn.t[:, :], [rn.b])
                    P.tt(rn.b, rn.t[:, :], rn.t[:, :], oraw[hd].t[:, :], ALU.mult, [rn.b, oraw[hd].b])
                    obT = obTs[hd % 2]
                    P.stt(obT.b, obT.t[:, :], rn.t[:, :], gdn.t[:, 0:1], zT[hd].t[:, :], ALU.mult, ALU.mult,
                          [rn.b, gdn.b, zT[hd].b])
                    for part in range(512 // CH):
                        ci, dst = oc_dst(512 + hd * 128, 128, tok0 + part * CH)
                        P.dma("sp", dst, obT.t[:, part * CH:(part + 1) * CH], obT.slot, [obT.b], [ocat_b[ci]])
                if t > 0:
                    for ci in range((t - 1) * (512 // CH), t * (512 // CH)):
                        gather_chunk(ci)

            if dbg == "cp7":
                return finish()
            for ci in range((NT1 - 1) * (512 // CH), NT1 * (512 // CH)):
                gather_chunk(ci)
            if dbg:
                dslot = P.slot()
                P.dma("pool", dbg_out, ogath.ap(), dslot, ogath_b, [])
                P.flush(final_waits=[(dslot.sem, dslot.count)])
                return nc
            P.flush()

        with ExitStack() as e2:
            big = P.sb(e2, [128, 68 * T2], BF16, slot=True)
            oT_t = big.t[:, 0:36 * T2].rearrange("p (c t) -> p c t", c=36)
            yT_t = big.t[:, 36 * T2:68 * T2].rearrange("p (c t) -> p c t", c=32)
            AH = 44
            actT_t = big.t[:, 0:AH * T2].rearrange("p (c t) -> p c t", c=AH)
            sg = P.sb(e2, [128, 4, T2], BF16)
            yacc = P.sb(e2, [128, 4, T2], F32)
            ytmp = P.sb(e2, [128, T2], F32)
            xst = [P.sb(e2, [128, 512], F32, slot=True) for _ in range(2)]
            ost = [P.sb(e2, [128, 512], F32, slot=True) for _ in range(2)]
            gst = [P.sb(e2, [128, 512], F32, slot=True) for _ in range(2)]
            segsb = P.sb(e2, [1, 1], I32, slot=True)
            out_b = Buf()
            sgate = P.sb(e2, [128, 4, T2], BF16)
            reg = e2.enter_context(nc.gpsimd.register("segreg"))
            P.dma("pool", segsb.t[:, :], seg, segsb.slot, [], [segsb.b])
            off_box = {}

            def ld_reg(e):
                ins = e.reg_load(reg, segsb.t[0:1, 0:1])
                off_box["v"] = e.snap(reg, min_val=0, max_val=(NCHK - OWN // CH) * 4608)
                return ins

            P.op("pool", ld_reg, [segsb.b], [])
            ogall = ogath.ap()
            wgv = w_gate.rearrange("(kc p) n -> p kc n", p=128)
            wbv = [w_br_a.rearrange("(kc p) n -> p kc n", p=128), w_br_b.rearrange("(kc p) n -> p kc n", p=128),
                   w_br_m.rearrange("(kc p) n -> p kc n", p=128)]
            wov = w_o.rearrange("(kc p) n -> p kc n", p=128)
            wfiv = w_fi.rearrange("(kc p) n -> p kc n", p=128)
            wfov = w_fo.rearrange("(kc p) n -> p kc n", p=128)
            x1v = x1scr.ap()
            xi = {"x": 0, "o": 0, "g": 0}

            def next_xst():
                i = xi["x"] % 2
                xi["x"] += 1
                return xst[i]

            def next_ost():
                i = xi["o"] % 2
                xi["o"] += 1
                return ost[i]

            br_k = [
                [(r * 4 + jj, r * 9 + jj) for r in range(4) for jj in range(4)],
                [(r * 4 + jj, r * 9 + 4 + jj) for r in range(4) for jj in range(4)],
                [(r, r * 9 + 8) for r in range(4)],
            ]

            for tt2 in range(NT2):
                r0 = tt2 * T2
                x1b = {(g, cg): Buf() for g in range(G2) for cg in range(8)}
                if tt2 == 0:
                    load_gfull(0)
                    for g in range(G2):
                        norm_transpose(x_own[r0 + g * 128:r0 + (g + 1) * 128, :], [], g, hT.t, hT.b)

                for part in range(T2 // CH):
                    lc = (r0 // CH) + part

                    def ld_o(e, lc=lc, part=part):
                        own_rows = ogall[bass.ds(off_box["v"], (OWN // CH) * 4608), :]
                        src = own_rows[lc * 4608:(lc + 1) * 4608, :].rearrange("(c p) t -> p c t", p=128)
                        return e.dma_start(out=oT_t[:, :, part * CH:(part + 1) * CH], in_=src)

                    P.custom("pool", ld_o, big.slot, 16, ogath_b, [big.b])

                rhs_h2 = lambda ri: hT.t[:, ri, 0:T2]
                rhs_o = lambda ri: oT_t[:, ri, :]
                for fg in range(8):
                    for br in range(3):
                        def ep_gate(cb, bank, br=br):
                            P.act(sg.b, sg.t[:, cb, :], bank.t[:, 0:T2], AF.Sigmoid, [bank.b])

                        stream_B(wgv, br * D + fg * 512, 512, KALL, rhs_h2, [hT.b], T2, ep_gate)

                        def ep_br(cb, bank, br=br, fg=fg):
                            if br == 0:
                                P.tt(yacc.b, yacc.t[:, cb, :], bank.t[:, 0:T2], sg.t[:, cb, :], ALU.mult, [bank.b, sg.b])
                            else:
                                P.tt(ytmp.b, ytmp.t[:, :], bank.t[:, 0:T2], sg.t[:, cb, :], ALU.mult, [bank.b, sg.b])
                                if br == 1:
                                    P.tt(yacc.b, yacc.t[:, cb, :], yacc.t[:, cb, :], ytmp.t[:, :], ALU.add, [yacc.b, ytmp.b])
                                else:
                                    P.tt(big.b, yT_t[:, fg * 4 + cb, :], yacc.t[:, cb, :], ytmp.t[:, :], ALU.add, [yacc.b, ytmp.b])

                        stream_B(wbv[br], fg * 512, 512, br_k[br], rhs_o, [big.b], T2, ep_br)

                for cg in range(8):
                    def ep_o(tg, bank, cg=cg):
                        xt = next_xst()
                        ot = next_ost()
                        rows = slice(r0 + tg * 128, r0 + (tg + 1) * 128)
                        P.dma("sp", xt.t[:, :], x_own[rows, cg * 512:(cg + 1) * 512], xt.slot, [], [xt.b])
                        P.tt(ot.b, ot.t[:, :], bank.t[:, :], xt.t[:, :], ALU.add, [bank.b, xt.b])
                        P.dma("sp", x1v[rows, cg * 512:(cg + 1) * 512], ot.t[:, :], ot.slot, [ot.b], [x1b[(tg, cg)]])

                    stream_A(wov, cg * 512, KALL, lambda ri, tg: yT_t[:, ri, tg * 128:(tg + 1) * 128], [big.b], G2, ep_o)

                load_gfull(2)
                for g in range(G2):
                    norm_transpose(x1v[r0 + g * 128:r0 + (g + 1) * 128, :], [x1b[(g, cg)] for cg in range(8)], g, hT.t, hT.b)

                for half, (c0, c1) in enumerate([(0, AH), (AH, FC)]):
                    nch = c1 - c0
                    for b0 in range(0, nch, 4):
                        nb = min(4, nch - b0)
                        col0 = (c0 + b0) * 128

                        def ep_gate2(cb, bank):
                            P.act(sgate.b, sgate.t[:, cb, :], bank.t[:, 0:T2], AF.Silu, [bank.b])

                        stream_B(wfiv, col0, nb * 128, KALL, rhs_h2, [hT.b], T2, ep_gate2)

                        def ep_up(cb, bank, b0=b0):
                            P.tt(big.b, actT_t[:, b0 + cb, :], bank.t[:, 0:T2], sgate.t[:, cb, :], ALU.mult, [bank.b, sgate.b])

                        stream_B(wfiv, D_FF + col0, nb * 128, KALL, rhs_h2, [hT.b], T2, ep_up)
                    kch = [(c0 + i, i) for i in range(nch)]
                    for cg in range(8):
                        if half == 1 and tt2 + 1 < NT2 and cg % 2 == 0 and cg // 2 < G2:
                            load_gfull(0)
                            nr0 = r0 + T2 + (cg // 2) * 128
                            norm_transpose(x_own[nr0:nr0 + 128, :], [], cg // 2, hT.t, hT.b)

                        def ep_f(tg, bank, cg=cg):
                            xt = next_xst()
                            ot = next_ost()
                            rows = slice(r0 + tg * 128, r0 + (tg + 1) * 128)
                            xb = x1b[(tg, cg)]
                            P.dma("sp", xt.t[:, :], x1v[rows, cg * 512:(cg + 1) * 512], xt.slot, [xb], [xt.b])
                            P.tt(ot.b, ot.t[:, :], bank.t[:, :], xt.t[:, :], ALU.add, [bank.b, xt.b])
                            P.dma("sp", x1v[rows, cg * 512:(cg + 1) * 512], ot.t[:, :], ot.slot, [ot.b], [xb])

                        stream_A(wfov, cg * 512, kch, lambda ri, tg: actT_t[:, ri, tg * 128:(tg + 1) * 128], [big.b], G2, ep_f)

                for g in range(G2):
                    rows = slice(r0 + g * 128, r0 + (g + 1) * 128)
                    xt = norm_group(x1v[rows, :], [x1b[(g, cg)] for cg in range(8)])
                    for q4 in range(8):
                        ot = next_ost()
                        gt = gst[xi["g"] % 2]
                        xi["g"] += 1
                        P.dma("sp", gt.t[:, :], gfin_in[:, q4 * 512:(q4 + 1) * 512], gt.slot, [], [gt.b])
                        P.stt(ot.b, ot.t[:, :], xt.t[:, q4 * 512:(q4 + 1) * 512], ss.t[:, 1:2], gt.t[:, :],
                              ALU.mult, ALU.mult, [xt.b, ss.b, gt.b])
                        P.dma("sp", out[rows, q4 * 512:(q4 + 1) * 512], ot.t[:, :], ot.slot, [ot.b], [out_b])

            finals = [tok for tok in out_b.w.values()]
            P.flush(final_waits=finals)
        print("ninst", P.ninst, "sbuf left", nc.sbuf_bytes_remaining, flush=True)
    return nc


def _t5_bucket_np(n):
    max_exact = 16
    nf = np.maximum(n, 1).astype(np.float32)
    large = max_exact + (np.log(nf / max_exact) / math.log(128 / max_exact) * (32 - max_exact)).astype(np.int32)
    large = np.minimum(large, 31)
    return np.where(n < max_exact, n, large)


def _prep_inputs(inp, S):
    f32 = np.float32
    x = np.asarray(inp["x"], f32)[:, :S]
    OWN = S // 4
    w_in = np.asarray(inp["w_in"], f32)[0]
    kk = np.arange(128)[:, None]
    qq = np.arange(128)[None, :]
    rel_bias = np.asarray(inp["rel_bias"], f32)
    tabs = []
    for kb in range(2):
        dist = qq - kk if kb == 1 else qq + 128 - kk
        valid = (dist >= 0) & (dist < 128)
        bidx = _t5_bucket_np(np.maximum(dist, 0))
        tabs.append((bidx, valid))
    cst = np.zeros((6, 128, 128), f32)
    ii = np.arange(128)[:, None]
    jj = np.arange(128)[None, :]
    cst[0] = np.eye(128, dtype=f32)
    cst[1] = (ii <= jj).astype(f32)
    cst[2] = 1.0
    cst[3] = np.where(jj <= ii, 0.0, -10000.0)
    cst[4] = (jj < ii).astype(f32)
    gcol = np.stack([np.asarray(inp[k], f32).reshape(KC, 128).T for k in ("g_mix", "g_mem", "g_ffn", "g_final")], axis=1)
    gcol = np.ascontiguousarray(gcol).astype(f32)
    gfin = np.ascontiguousarray(np.broadcast_to(np.asarray(inp["g_final"], f32).reshape(1, D), (128, D))).astype(f32)
    w_gate = np.ascontiguousarray(w_in[:, GATE0:])
    shared = {
        "w_gate": w_gate,
        "w_br_a": np.asarray(inp["w_br_a"], f32)[0], "w_br_b": np.asarray(inp["w_br_b"], f32)[0],
        "w_br_m": np.asarray(inp["w_br_m"], f32)[0], "w_o": np.asarray(inp["w_o"], f32)[0],
        "w_fi": np.asarray(inp["w_ffn_in"], f32)[0], "w_fo": np.asarray(inp["w_ffn_out"], f32)[0],
        "gcol": gcol, "gfin": gfin, "cst": cst,
        "gdn": np.asarray(inp["g_dn_out"], f32).reshape(128, 1).copy(),
    }
    conv_w = np.asarray(inp["conv_w"], f32)[0]
    w_mem = np.asarray(inp["w_mem_kv"], f32)[0]
    sinks = np.asarray(inp["sinks"], f32)[0]
    a_log = np.asarray(inp["a_log"], f32)[0]
    dt_bias = np.asarray(inp["dt_bias"], f32)[0]
    offs = np.cumsum([0, 2048, 256, 256, 6144, 2048, 16, 16, 512])
    oAq, oAk, oAv, oB, oZ, oBb, oBa, oMq = offs[:8]
    maps = []
    for c in range(8):
        b, hg = c // 4, c % 4
        cols = []
        cols += list(range(oAq + hg * 512, oAq + (hg + 1) * 512))
        kc_ = list(range(oAk + hg * 64, oAk + (hg + 1) * 64))
        cols += kc_ + kc_
        cols += list(range(oMq + hg * 128, oMq + (hg + 1) * 128))
        cols += list(range(oAv + hg * 64, oAv + (hg + 1) * 64))
        cols += list(range(oBb + hg * 4, oBb + (hg + 1) * 4))
        cols += list(range(oBa + hg * 4, oBa + (hg + 1) * 4))
        npad = 1024 - len(cols)
        cols1 = np.array(cols)
        w1 = np.zeros((D, NC1), f32)
        w1[:, :len(cols)] = w_in[:, cols1]
        for kind in range(3):
            c0 = oB + kind * 2048 + hg * 512
            w1[:, 1024 + kind * 512:1024 + (kind + 1) * 512] = w_in[:, c0:c0 + 512]
        w1[:, 2560:3072] = w_in[:, oZ + hg * 512: oZ + (hg + 1) * 512]
        convw = np.zeros((128, 12, 4), f32)
        for kind in range(3):
            for hd in range(4):
                c0 = kind * 2048 + hg * 512 + hd * 128
                convw[:, kind * 4 + hd, :] = conv_w[:, c0:c0 + 128].T
        biasT = np.zeros((128, 2, 8, 128), f32)
        for kb in range(2):
            bidx, valid = tabs[kb]
            for h in range(8):
                tab = rel_bias[bidx, hg * 8 + h]
                biasT[:, kb, h, :] = np.where(valid, tab, NEGM)
        sinkrep = np.broadcast_to(np.repeat(sinks[hg * 8:(hg + 1) * 8], 128)[None, :], (64, 1024)).astype(f32)
        m = dict(shared)
        m.update({
            "x_full": np.ascontiguousarray(x[b]),
            "x_own": np.ascontiguousarray(x[b, hg * OWN:(hg + 1) * OWN]),
            "seg": np.array([[hg * (OWN // min(256, OWN)) * 4608]], np.int32),
            "mem": np.ascontiguousarray(np.asarray(inp["mem"], f32)[b]),
            "w1": w1,
            "w_mkv": np.ascontiguousarray(np.concatenate([w_mem[:, hg * 128:(hg + 1) * 128],
                                                          w_mem[:, 512 + hg * 128:512 + (hg + 1) * 128]], axis=1)),
            "biasT": biasT.reshape(128, 2048),
            "sinkrep": np.ascontiguousarray(sinkrep),
            "convw": convw.reshape(128, 48),
            "alogrep": np.ascontiguousarray(np.broadcast_to(np.tile(a_log[hg * 4:(hg + 1) * 4], 4)[None, :], (128, 16))).astype(f32),
            "dtbrep": np.ascontiguousarray(np.broadcast_to(np.tile(dt_bias[hg * 4:(hg + 1) * 4], 4)[None, :], (128, 16))).astype(f32),
        })
        maps.append(m)
    return maps


_NC_CACHE = {}


def run(inp, S):
    if S not in _NC_CACHE:
        _NC_CACHE[S] = build(S)
    nc = _NC_CACHE[S]
    maps = _prep_inputs(inp, S)
    res = run_bass_kernel_spmd(nc, maps, core_ids=list(range(8)))
    OWN = S // 4
    outp = np.zeros((2, S, D), np.float32)
    for c in range(8):
        b, hg = c // 4, c % 4
        outp[b, hg * OWN:(hg + 1) * OWN] = np.asarray(res.results[c]["out"])
    return outp


def kernel(**inputs):
    return run(inputs, 8192)
```

```python
import math
from contextlib import ExitStack

import numpy as np
import ml_dtypes

import concourse.bass as bass
import concourse.mybir as mybir
from concourse.bass_utils import run_bass_kernel_spmd

F32 = mybir.dt.float32
BF16 = mybir.dt.bfloat16
I32 = mybir.dt.int32
AF = mybir.ActivationFunctionType
ALU = mybir.AluOpType

D = 4096
KC = D // 128
D_FF = 11008
FC = D_FF // 128
N_IN = 23584
GATE0 = N_IN - 3 * D
NSLOT = 4
EPS = 1e-6
NEGM = -30000.0
NC1 = 24 * 128


class Buf:
    __slots__ = ("w", "r", "excl")

    def __init__(self):
        self.w = {}
        self.r = {}
        self.excl = False


def _merge(d, tok):
    s, v = tok
    k = id(s)
    if k not in d or d[k][1] < v:
        d[k] = (s, v)


class Slot:
    def __init__(self, sem):
        self.sem = sem
        self.count = 0


class TT:
    def __init__(self, t):
        self.t = t
        self.b = Buf()
        self.slot = None


class Prog:
    ENG = ("pe", "act", "dve", "pool", "sp")

    def __init__(self, nc, es):
        self.nc = nc
        self.es = es
        self.q = {k: [] for k in self.ENG}
        self.sem = {k: es.enter_context(nc.semaphore("s_" + k)) for k in self.ENG}
        self.cnt = {k: 0 for k in self.ENG}
        self.waited = {k: {} for k in self.ENG}
        self.nslots = 0
        self.ninst = 0
        self.nt = 0

    def slot(self):
        self.nslots += 1
        sl = Slot(self.es.enter_context(self.nc.semaphore("d%d" % self.nslots)))
        if not hasattr(self, "slot_of"):
            self.slot_of = {}
        self.slot_of[id(sl.sem)] = sl
        return sl

    def sb(self, es, shape, dt, slot=False):
        self.nt += 1
        t = TT(es.enter_context(self.nc.sbuf_tensor("t%d" % self.nt, shape, dt)))
        if slot:
            t.slot = self.slot()
        return t

    def ps(self, es, shape, dt):
        self.nt += 1
        t = TT(es.enter_context(self.nc.psum_tensor("p%d" % self.nt, shape, dt)))
        t.b.excl = True
        return t

    def _deps(self, eng, reads, writes):
        need = {}
        for b in reads:
            for tok in b.w.values():
                _merge(need, tok)
            if b.excl:
                for tok in b.r.values():
                    _merge(need, tok)
        for b in writes:
            for tok in b.w.values():
                _merge(need, tok)
            for tok in b.r.values():
                _merge(need, tok)
        own = id(self.sem[eng])
        w = self.waited[eng]
        out = []
        slot_of = getattr(self, "slot_of", {})
        for k, (s, v) in need.items():
            if eng == "pe" and k == own:
                continue
            if k in slot_of:
                v = slot_of[k].count
            if w.get(k, 0) >= v:
                continue
            w[k] = v
            out.append((s, v))
        return out

    def _mark(self, tok, reads, writes):
        for b in reads:
            _merge(b.r, tok)
        for b in writes:
            _merge(b.w, tok)

    def op(self, eng, fn, reads=(), writes=()):
        waits = self._deps(eng, reads, writes)
        self.cnt[eng] += 1
        s = self.sem[eng]
        tok = (s, self.cnt[eng])
        self.q[eng].append((fn, waits, s, 1))
        self.ninst += 1 + len(waits)
        self._mark(tok, reads, writes)
        return tok

    def dma(self, eng, out, in_, slot, reads=(), writes=()):
        waits = self._deps(eng, reads, writes)
        slot.count += 16
        tok = (slot.sem, slot.count)
        self.q[eng].append((lambda e: e.dma_start(out=out, in_=in_), waits, slot.sem, 16))
        self.ninst += 1 + len(waits)
        self._mark(tok, reads, writes)
        return tok

    def custom(self, eng, fn, slot, inc, reads=(), writes=()):
        waits = self._deps(eng, reads, writes)
        slot.count += inc
        tok = (slot.sem, slot.count)
        self.q[eng].append((fn, waits, slot.sem, inc))
        self._mark(tok, reads, writes)
        return tok

    def flush(self, final_waits=()):
        nc = self.nc
        q = self.q

        def run(e, items):
            for fn, waits, s, n in items:
                for (ws, wv) in waits:
                    e.wait_ge(ws, wv)
                ins = fn(e)
                if s is not None:
                    ins.then_inc(s, n)

        with nc.Block() as block:
            @block.tensor
            def _(e):
                run(e, q["pe"])

            @block.scalar
            def _(e):
                run(e, q["act"])

            @block.vector
            def _(e):
                run(e, q["dve"])

            @block.gpsimd
            def _(e):
                run(e, q["pool"])

            @block.sync
            def _(e):
                run(e, q["sp"])
                for (ws, wv) in final_waits:
                    e.wait_ge(ws, wv)

        self.q = {k: [] for k in self.ENG}

    def mm(self, ob, out, lhsT, rhs, rd, start=True, stop=True):
        return self.op("pe", lambda e: e.matmul(out, lhsT=lhsT, rhs=rhs, start=start, stop=stop), rd, [ob])

    def tr(self, ob, out, in_, ident, rd):
        return self.op("pe", lambda e: e.transpose(out, in_, ident), rd, [ob])

    def act(self, ob, out, in_, func, rd, bias=None, scale=None, accum=None, eng="act"):
        kw = {}
        if bias is not None:
            kw["bias"] = bias
        if scale is not None:
            kw["scale"] = scale
        if accum is not None:
            kw["accum_out"] = accum
        wr = ob if isinstance(ob, (list, tuple)) else [ob]
        return self.op("act", lambda e: e.activation(out=out, in_=in_, func=func, **kw), rd, wr)

    def copy(self, eng, ob, out, in_, rd):
        if eng == "act":
            return self.op("act", lambda e: e.activation(out=out, in_=in_, func=AF.Copy), rd, [ob])
        return self.op(eng, lambda e: e.tensor_copy(out=out, in_=in_), rd, [ob])

    def tt(self, ob, out, in0, in1, op, rd, eng="dve"):
        return self.op(eng, lambda e: e.tensor_tensor(out=out, in0=in0, in1=in1, op=op), rd, [ob])

    def ts(self, ob, out, in0, s1, op0, rd, s2=None, op1=None, eng="dve"):
        if op1 is None:
            return self.op(eng, lambda e: e.tensor_scalar(out=out, in0=in0, scalar1=s1, scalar2=None, op0=op0), rd, [ob])
        return self.op(eng, lambda e: e.tensor_scalar(out=out, in0=in0, scalar1=s1, scalar2=s2, op0=op0, op1=op1), rd, [ob])

    def stt(self, ob, out, in0, scalar, in1, op0, op1, rd, eng="dve"):
        return self.op(eng, lambda e: e.scalar_tensor_tensor(out=out, in0=in0, scalar=scalar, in1=in1, op0=op0, op1=op1), rd, [ob])

    def recip(self, ob, out, in_, rd):
        return self.op("dve", lambda e: e.reciprocal(out=out, in_=in_), rd, [ob])

    def memset(self, eng, ob, ap, val):
        return self.op(eng, lambda e: e.memset(ap, val), [], [ob])


def build(S, dbg=False):
    OWN = S // 4
    T1 = 512
    NT1 = S // T1
    T2 = min(512, OWN)
    NT2 = OWN // T2
    G2 = T2 // 128

    nc = bass.Bass("TRN2", target_bir_lowering=False)

    def din(name, shape, dt=F32):
        return nc.dram_tensor(name, list(shape), dt, kind="ExternalInput").ap()

    x_full = din("x_full", [S, D])
    x_own = din("x_own", [OWN, D])
    seg = din("seg", [1, 1], I32)
    mem = din("mem", [256, D])
    w1 = din("w1", [D, NC1])
    w_mkv = din("w_mkv", [D, 256])
    if not dbg:
        w_gate = din("w_gate", [D, 3 * D])
        w_br_a = din("w_br_a", [2048, D])
        w_br_b = din("w_br_b", [2048, D])
        w_br_m = din("w_br_m", [512, D])
        w_o = din("w_o", [D, D])
        w_fi = din("w_fi", [D, 2 * D_FF])
        w_fo = din("w_fo", [D_FF, D])
    else:
        dbg_out = nc.dram_tensor("dbg", [(S // min(256, S // 4)) * 4 * 1152, min(256, S // 4)], BF16, kind="ExternalOutput").ap()
    gcol_in = din("gcol", [128, 4, KC])
    gfin_in = din("gfin", [128, D])
    biasT_in = din("biasT", [128, 2 * 8 * 128])
    sinkrep_in = din("sinkrep", [64, 1024])
    convw_in = din("convw", [128, 48])
    gdn_in = din("gdn", [128, 1])
    alog_in = din("alogrep", [128, 16])
    dtb_in = din("dtbrep", [128, 16])
    cst_in = din("cst", [6, 128, 128])
    out = nc.dram_tensor("out", [OWN, D], F32, kind="ExternalOutput").ap()

    CH = min(256, OWN)
    NCHK = S // CH
    ocat = nc.dram_tensor("ocat", [NCHK * 1152, CH], BF16)
    ogath = nc.dram_tensor("ogath", [NCHK * 4 * 1152, CH], BF16)
    x1scr = nc.dram_tensor("x1scr", [OWN, D], F32)

    with ExitStack() as es:
        P = Prog(nc, es)
        ident_f = P.sb(es, [128, 128], F32)
        ident_b = P.sb(es, [128, 128], BF16)
        U_f = P.sb(es, [128, 128], F32)
        ones_f = P.sb(es, [128, 128], F32)
        ones_b = P.sb(es, [128, 128], BF16)
        maskneg = P.sb(es, [128, 128], F32)
        mstrict = P.sb(es, [128, 128], F32)
        gcol = P.sb(es, [128, 4, KC], F32, slot=True)
        hb = P.sb(es, [128, D], BF16)
        ss = P.sb(es, [128, 2], F32)
        xs = [P.sb(es, [128, D], F32, slot=True) for _ in range(1)]
        hT = P.sb(es, [128, KC, 512], BF16)
        wsl = [P.sb(es, [128, 8, 512], BF16, slot=True) for _ in range(NSLOT)]
        psf = [P.ps(es, [128, 512], F32) for _ in range(6)]
        psb = [P.ps(es, [128, 8, 128], BF16) for _ in range(2)]
        cslot = P.slot()
        st = {"wi": 0, "pf": 0, "pb": 0, "xi": 0}

        def next_w():
            i = st["wi"] % NSLOT
            st["wi"] += 1
            return wsl[i]

        def next_pf():
            i = st["pf"] % 6
            st["pf"] += 1
            return psf[i]

        def next_pb():
            i = st["pb"] % 2
            st["pb"] += 1
            return psb[i]

        cbufs = [ident_f, U_f, ones_f, maskneg, mstrict]
        for i, tt_ in enumerate(cbufs):
            P.dma("sp", tt_.t[:, :], cst_in[i, :, :], cslot, [], [tt_.b])
        cslot2 = P.slot()
        P.dma("pool", ident_b.t[:, :], cst_in[0, :, :], cslot2, [], [ident_b.b])
        P.dma("pool", ones_b.t[:, :], cst_in[2, :, :], cslot2, [], [ones_b.b])
        ctok = (cslot.sem, cslot.count)
        for tt_ in cbufs:
            tt_.b.w = {id(cslot.sem): ctok}
        ctok2 = (cslot2.sem, cslot2.count)
        for tt_ in [ident_b, ones_b]:
            tt_.b.w = {id(cslot2.sem): ctok2}

        P.dma("sp", gcol.t[:, :, :], gcol_in, gcol.slot, [], [gcol.b])
        st["gi"] = 0

        def load_gfull(i):
            st["gi"] = i

        def norm_group(src_ap, src_bufs):
            xt = xs[0]
            st["xi"] += 1
            P.dma("sp", xt.t[:, :], src_ap, xt.slot, src_bufs, [xt.b])
            P.op("act", lambda e: e.memzero(ss.t[:, 0:1]), [], [ss.b])
            P.act([hb.b, ss.b], hb.t[:, :], xt.t[:, :], AF.Square, [xt.b], accum=ss.t[:, 0:1])
            P.act(ss.b, ss.t[:, 1:2], ss.t[:, 0:1], AF.Sqrt, [ss.b], bias=EPS, scale=1.0 / D)
            P.recip(ss.b, ss.t[:, 1:2], ss.t[:, 1:2], [ss.b])
            return xt

        def norm_transpose(src_ap, src_bufs, g, dst, dstb):
            xt = norm_group(src_ap, src_bufs)
            P.act(hb.b, hb.t[:, :], xt.t[:, :], AF.Copy, [xt.b, ss.b], scale=ss.t[:, 1:2])
            gi = st["gi"]
            for q4 in range(4):
                pb = next_pb()
                for k8 in range(8):
                    k = q4 * 8 + k8
                    P.tr(pb.b, pb.t[:, k8, :], hb.t[:, k * 128:(k + 1) * 128], ident_b.t[:, :], [hb.b, ident_b.b])
                for k8 in range(8):
                    k = q4 * 8 + k8
                    if k8 % 2 == 0:
                        P.act(dstb, dst[:, k, g * 128:(g + 1) * 128], pb.t[:, k8, :], AF.Copy, [pb.b, gcol.b],
                              scale=gcol.t[:, gi, k:k + 1])
                    else:
                        P.ts(dstb, dst[:, k, g * 128:(g + 1) * 128], pb.t[:, k8, :], gcol.t[:, gi, k:k + 1], ALU.mult,
                             [pb.b, gcol.b])

        def stream_B(wsrc, col0, ncols, kchunks, rhs_fn, rhs_bufs, ntok, epilogue):
            nblk = (ncols + 127) // 128
            banks = [next_pf() for _ in range(nblk)]
            nk = len(kchunks)
            for s0 in range(0, nk, 8):
                sub = kchunks[s0:s0 + 8]
                wt = next_w()
                runs = []
                for i, (wk, _) in enumerate(sub):
                    if runs and runs[-1][1] + runs[-1][2] == wk:
                        runs[-1][2] += 1
                    else:
                        runs.append([i, wk, 1])
                for (i0, wk0, n) in runs:
                    P.dma("pool", wt.t[:, i0:i0 + n, 0:ncols], wsrc[:, wk0:wk0 + n, col0:col0 + ncols], wt.slot, [], [wt.b])
                for cb in range(nblk):
                    cw = min(128, ncols - cb * 128)
                    for i, (wk, ri) in enumerate(sub):
                        kk = s0 + i
                        P.mm(banks[cb].b, banks[cb].t[0:cw, 0:ntok], wt.t[:, i, cb * 128:cb * 128 + cw], rhs_fn(ri),
                             [wt.b] + rhs_bufs, start=(kk == 0), stop=(kk == nk - 1))
            for cb in range(nblk):
                epilogue(cb, banks[cb])

        def stream_A(wsrc, col0, kchunks, lhs_fn, lhs_bufs, ngrp, epilogue):
            banks = [next_pf() for _ in range(ngrp)]
            nk = len(kchunks)
            for s0 in range(0, nk, 8):
                sub = kchunks[s0:s0 + 8]
                wt = next_w()
                wk0 = sub[0][0]
                n = len(sub)
                P.dma("pool", wt.t[:, 0:n, :], wsrc[:, wk0:wk0 + n, col0:col0 + 512], wt.slot, [], [wt.b])
                for tg in range(ngrp):
                    for i, (wk, ri) in enumerate(sub):
                        kk = s0 + i
                        P.mm(banks[tg].b, banks[tg].t[:, :], lhs_fn(ri, tg), wt.t[:, i, :],
                             [wt.b] + lhs_bufs, start=(kk == 0), stop=(kk == nk - 1))
            for tg in range(ngrp):
                epilogue(tg, banks[tg])

        KALL = [(k, k) for k in range(KC)]

        with ExitStack() as e1:
            mkT = P.sb(e1, [128, 256], BF16)
            mvt = P.sb(e1, [128, 2, 128], BF16)
            biasT = P.sb(e1, [128, 2, 1024], BF16, slot=True)
            sinkexp = P.sb(e1, [64, 1024], F32, slot=True)
            convw = P.sb(e1, [128, 48], F32, slot=True)
            gdn = P.sb(e1, [128, 1], F32, slot=True)
            nea = P.sb(e1, [128, 16], F32, slot=True)
            dtb = P.sb(e1, [128, 16], F32, slot=True)
            qT = P.sb(e1, [128, 4, 512], BF16)
            kbuf = P.sb(e1, [128, 640], BF16)
            mqT = P.sb(e1, [128, 512], BF16)
            vT = P.sb(e1, [64, 512], BF16)
            ba_sb = P.sb(e1, [128, 512], F32)
            vtok = P.sb(e1, [128, 5, 64], BF16)
            ba_tok = P.sb(e1, [128, 4, 8], F32)
            beta_t = P.sb(e1, [128, 4, 4], F32)
            g_t = P.sb(e1, [128, 4, 4], F32)
            sp1 = P.sb(e1, [128, 4, 4], F32)
            sp2 = P.sb(e1, [128, 4, 4], F32)
            gcg = P.sb(e1, [128, 32], F32)
            egc = P.sb(e1, [128, 16], F32)
            bege = P.sb(e1, [128, 16], F32)
            ek2 = P.sb(e1, [128, 16], F32)
            egt = P.sb(e1, [128, 16], F32)
            raw = [P.sb(e1, [128, 515], F32) for _ in range(4)]
            hist = P.sb(e1, [128, 12, 3], F32)
            cacc = [P.sb(e1, [128, 512], F32) for _ in range(1)]
            slb = [P.sb(e1, [128, 512], F32) for _ in range(1)]
            sqb = [P.sb(e1, [128, 512], BF16) for _ in range(1)]
            rnb = [P.sb(e1, [128, 512], F32) for _ in range(1)]
            qh = [P.sb(e1, [128, 512], BF16) for _ in range(4)]
            kh = [P.sb(e1, [128, 512], BF16) for _ in range(4)]
            vg = [P.sb(e1, [128, 512], BF16) for _ in range(4)]
            zT = [P.sb(e1, [128, 512], BF16) for _ in range(4)]
            oraw = [P.sb(e1, [128, 512], F32) for _ in range(4)]
            obTs = [P.sb(e1, [128, 512], BF16, slot=True) for _ in range(2)]
            oaTs = [P.sb(e1, [64, 8, 128], BF16, slot=True) for _ in range(2)]
            omT = P.sb(e1, [128, 512], BF16, slot=True)
            Sf = [P.sb(e1, [128, 128], F32) for _ in range(4)]
            Sb = [P.sb(e1, [128, 128], BF16) for _ in range(4)]
            pT = [P.sb(e1, [128, 1024], BF16) for _ in range(2)]
            pmT = P.sb(e1, [128, 2, 512], BF16)
            den = P.sb(e1, [128, 512], F32)
            ocat_b = [Buf() for _ in range(NCHK)]
            ogath_b = [Buf() for _ in range(NCHK)]
            ccs = P.slot()

            def oc_dst(row0, nrows, tokabs):
                ci = tokabs // CH
                return ci, ocv[ci * 1152 + row0:ci * 1152 + row0 + nrows, :]

            def gather_chunk(ci):
                P.custom("pool", lambda e, ci=ci: e.collective_compute(
                    "AllGather", ALU.bypass, replica_groups=[[0, 1, 2, 3], [4, 5, 6, 7]],
                    ins=[ocat.ap()[ci * 1152:(ci + 1) * 1152, :].opt()],
                    outs=[ogath.ap()[ci * 4608:(ci + 1) * 4608, :].opt()]),
                    ccs, 1, [ocat_b[ci]], [ogath_b[ci]])
            print("sbuf left before ctx", nc.sbuf_bytes_remaining, flush=True)

            class Ctx:
                pass

            NCH = 4
            ctxs = []
            for i in range(NCH):
                c = Ctx()
                c.gB = P.sb(e1, [128, 128], F32)
                c.X = P.sb(e1, [128, 128], F32)
                c.Dm = c.X
                c.Dst = P.sb(e1, [128, 128], F32)
                c.Egc = P.sb(e1, [128, 128], BF16)
                c.qt = P.sb(e1, [128, 128], BF16)
                c.L = [P.sb(e1, [128, 256], F32) for _ in range(2)]
                c.att = P.sb(e1, [128, 128], BF16)
                c.attT = P.sb(e1, [128, 128], BF16)
                c.r = [P.sb(e1, [128, 256], F32) for _ in range(2)]
                c.k2 = P.sb(e1, [128, 128], BF16)
                c.wT = P.sb(e1, [128, 128], BF16)
                c.vn = P.sb(e1, [128, 128], BF16)
                ctxs.append(c)

            P.dma("pool", biasT.t[:, :, :], biasT_in.rearrange("p (k n) -> p k n", k=2), biasT.slot, [], [biasT.b])
            P.dma("sp", sinkexp.t[:, :], sinkrep_in, sinkexp.slot, [], [sinkexp.b])
            P.act(sinkexp.b, sinkexp.t[:, :], sinkexp.t[:, :], AF.Exp, [sinkexp.b])
            P.dma("sp", convw.t[:, :], convw_in, convw.slot, [], [convw.b])
            P.dma("sp", gdn.t[:, :], gdn_in, gdn.slot, [], [gdn.b])
            P.dma("sp", nea.t[:, :], alog_in, nea.slot, [], [nea.b])
            P.dma("sp", dtb.t[:, :], dtb_in, dtb.slot, [], [dtb.b])
            P.act(nea.b, nea.t[:, :], nea.t[:, :], AF.Exp, [nea.b])
            P.ts(nea.b, nea.t[:, :], nea.t[:, :], -1.0, ALU.mult, [nea.b])
            P.memset("dve", hist.b, hist.t[:, :, :], 0.0)
            for hd in range(4):
                P.memset("dve", Sf[hd].b, Sf[hd].t[:, :], 0.0)
                P.memset("dve", Sb[hd].b, Sb[hd].t[:, :], 0.0)

            def finish():
                dslot = P.slot()
                P.dma("pool", dbg_out[0:NCHK * 1152, :], ocat.ap(), dslot, ocat_b, [])
                P.flush(final_waits=[(dslot.sem, dslot.count)])
                return nc

            if dbg == "cp1":
                return finish()
            load_gfull(1)
            for g in range(2):
                norm_transpose(mem[g * 128:(g + 1) * 128, :], [], g, hT.t, hT.b)
            wmv = w_mkv.rearrange("(kc p) n -> p kc n", p=128)

            def ep_mk(cb, bank):
                P.copy("act", mkT.b, mkT.t[:, :], bank.t[:, 0:256], [bank.b])

            stream_B(wmv, 0, 128, KALL, lambda ri: hT.t[:, ri, 0:256], [hT.b], 256, ep_mk)
            mvb = [next_pf() for _ in range(2)]
            for s0 in range(0, KC, 8):
                wt = next_w()
                P.dma("pool", wt.t[:, 0:8, 0:128], wmv[:, s0:s0 + 8, 128:256], wt.slot, [], [wt.b])
                for mg in range(2):
                    for i in range(8):
                        k = s0 + i
                        P.mm(mvb[mg].b, mvb[mg].t[:, 0:128], hT.t[:, k, mg * 128:(mg + 1) * 128], wt.t[:, i, 0:128],
                             [wt.b, hT.b], start=(k == 0), stop=(k == KC - 1))
            for mg in range(2):
                P.copy("act", mvt.b, mvt.t[:, mg, :], mvb[mg].t[:, 0:128], [mvb[mg].b])

            if dbg == "cp2":
                return finish()
            load_gfull(0)
            w1v = w1.rearrange("(kc p) n -> p kc n", p=128)
            ocv = ocat.ap()

            for t in range(NT1):
                tok0 = t * T1
                for g in range(4):
                    norm_transpose(x_full[tok0 + g * 128: tok0 + (g + 1) * 128, :], [], g, hT.t, hT.b)

                if dbg == "cp3":
                    return finish()
                rhs_h = lambda ri: hT.t[:, ri, :]

                def ep_q(cb, bank):
                    P.act(qT.b, qT.t[:, cb, :], bank.t[:, :], AF.Copy, [bank.b], scale=0.125)

                stream_B(w1v, 0, 512, KALL, rhs_h, [hT.b], 512, ep_q)

                def ep_g1(cb, bank):
                    if cb == 0:
                        P.copy("act", kbuf.b, kbuf.t[:, 128:640], bank.t[:, :], [bank.b])
                    elif cb == 1:
                        P.act(mqT.b, mqT.t[:, :], bank.t[:, :], AF.Copy, [bank.b], scale=128.0 ** -0.5)
                    else:
                        P.copy("dve", vT.b, vT.t[:, :], bank.t[0:64, :], [bank.b])
                        P.copy("dve", ba_sb.b, ba_sb.t[64:72, :], bank.t[64:72, :], [bank.b])

                stream_B(w1v, 512, 328, KALL, rhs_h, [hT.b], 512, ep_g1)

                pb = next_pb()
                for j in range(4):
                    P.tr(pb.b, pb.t[:, j, 0:64], vT.t[0:64, j * 128:(j + 1) * 128], ident_b.t[0:64, 0:64], [vT.b, ident_b.b])
                P.copy("act", vtok.b, vtok.t[:, 1:5, :], pb.t[:, 0:4, 0:64], [pb.b])
                pf = next_pf()
                for j in range(4):
                    P.tr(pf.b, pf.t[:, j * 8:(j + 1) * 8], ba_sb.t[64:72, j * 128:(j + 1) * 128], ident_f.t[64:72, 64:72],
                         [ba_sb.b, ident_f.b])
                P.copy("dve", ba_tok.b, ba_tok.t[:, :, :], pf.t[:, 0:32].rearrange("p (j c) -> p j c", j=4), [pf.b])
                P.act(beta_t.b, beta_t.t[:, :, :], ba_tok.t[:, :, 0:4], AF.Sigmoid, [ba_tok.b])
                dtb3 = dtb.t[:, :].rearrange("p (j c) -> p j c", j=4)
                nea3 = nea.t[:, :].rearrange("p (j c) -> p j c", j=4)
                P.tt(sp1.b, sp1.t[:, :, :], ba_tok.t[:, :, 4:8], dtb3, ALU.add, [ba_tok.b, dtb.b])
                P.ts(sp2.b, sp2.t[:, :, :], sp1.t[:, :, :], -1.0, ALU.mult, [sp1.b])
                P.tt(sp2.b, sp2.t[:, :, :], sp2.t[:, :, :], sp1.t[:, :, :], ALU.max, [sp1.b, sp2.b])
                P.act(sp2.b, sp2.t[:, :, :], sp2.t[:, :, :], AF.Exp, [sp2.b], scale=-1.0)
                P.act(sp2.b, sp2.t[:, :, :], sp2.t[:, :, :], AF.Ln, [sp2.b], bias=1.0)
                P.stt(sp1.b, sp1.t[:, :, :], sp1.t[:, :, :], 0.0, sp2.t[:, :, :], ALU.max, ALU.add, [sp1.b, sp2.b])
                P.tt(g_t.b, g_t.t[:, :, :], sp1.t[:, :, :], nea3, ALU.mult, [sp1.b, nea.b])
                pf = next_pf()
                g16 = g_t.t[:, :, :].rearrange("p j c -> p (j c)")
                P.mm(pf.b, pf.t[:, 0:16], U_f.t[:, :], g16, [U_f.b, g_t.b])
                P.mm(pf.b, pf.t[:, 16:32], ones_f.t[:, :], g16, [ones_f.b, g_t.b])
                P.copy("dve", gcg.b, gcg.t[:, :], pf.t[:, 0:32], [pf.b])
                P.act(egc.b, egc.t[:, :], gcg.t[:, 0:16], AF.Exp, [gcg.b])
                P.act(egt.b, egt.t[:, :], gcg.t[:, 16:32], AF.Exp, [gcg.b])
                P.tt(ek2.b, ek2.t[:, :], gcg.t[:, 16:32], gcg.t[:, 0:16], ALU.subtract, [gcg.b])
                P.act(ek2.b, ek2.t[:, :], ek2.t[:, :], AF.Exp, [ek2.b])
                P.tt(bege.b, bege.t[:, :], beta_t.t[:, :, :].rearrange("p j c -> p (j c)"), egc.t[:, :], ALU.mult, [beta_t.b, egc.b])

                def ep_raw(cb, bank):
                    r = raw[cb]
                    P.copy("act" if cb % 2 == 0 else "dve", r.b, r.t[:, 3:515], bank.t[:, :], [bank.b])

                def conv_group(kind):
                    for hd in range(4):
                        blk = kind * 4 + hd
                        r = raw[hd]
                        ca = cacc[0]
                        P.copy("dve", r.b, r.t[:, 0:3], hist.t[:, blk, :], [hist.b])
                        P.ts(ca.b, ca.t[:, :], r.t[:, 0:512], convw.t[:, blk * 4:blk * 4 + 1], ALU.mult, [r.b, convw.b])
                        for i in range(1, 4):
                            P.stt(ca.b, ca.t[:, :], r.t[:, i:i + 512], convw.t[:, blk * 4 + i:blk * 4 + i + 1], ca.t[:, :],
                                  ALU.mult, ALU.add, [r.b, convw.b, ca.b])
                        P.copy("dve", hist.b, hist.t[:, blk, :], r.t[:, 512:515], [r.b])
                        if kind == 2:
                            P.act(vg[hd].b, vg[hd].t[:, :], ca.t[:, :], AF.Silu, [ca.b])
                            continue
                        sl = slb[0]
                        sq = sqb[0]
                        rn = rnb[0]
                        P.act(sl.b, sl.t[:, :], ca.t[:, :], AF.Silu, [ca.b])
                        P.act(sq.b, sq.t[:, :], sl.t[:, :], AF.Square, [sl.b])
                        pf = next_pf()
                        P.mm(pf.b, pf.t[:, :], ones_b.t[:, :], sq.t[:, :], [ones_b.b, sq.b])
                        P.act(rn.b, rn.t[:, :], pf.t[:, :], AF.Sqrt, [pf.b], bias=EPS)
                        P.recip(rn.b, rn.t[:, :], rn.t[:, :], [rn.b])
                        if kind == 0:
                            P.stt(qh[hd].b, qh[hd].t[:, :], sl.t[:, :], 128.0 ** -0.5, rn.t[:, :], ALU.mult, ALU.mult, [sl.b, rn.b])
                        else:
                            P.tt(kh[hd].b, kh[hd].t[:, :], sl.t[:, :], rn.t[:, :], ALU.mult, [sl.b, rn.b])

                for gi in range(3):
                    stream_B(w1v, 1024 + gi * 512, 512, KALL, rhs_h, [hT.b], 512, ep_raw)
                    conv_group(gi)

                def ep_z(cb, bank):
                    P.act(zT[cb].b, zT[cb].t[:, :], bank.t[:, :], AF.Silu, [bank.b])

                stream_B(w1v, 2560, 512, KALL, rhs_h, [hT.b], 512, ep_z)

                if dbg == "cp4":
                    return finish()
                for j in range(4):
                    nblk = t * 4 + j
                    kbs = [1] if nblk == 0 else [0, 1]
                    sT = {}
                    for kb in kbs:
                        kcol = (j + kb) * 128
                        banks = [next_pf(), next_pf()]
                        for half in range(2):
                            bk = banks[half]
                            for h4 in range(4):
                                h = half * 4 + h4
                                p0 = (h % 2) * 64
                                P.mm(bk.b, bk.t[:, h4 * 128:(h4 + 1) * 128], ident_b.t[:, :],
                                     biasT.t[:, kb, h * 128:(h + 1) * 128], [ident_b.b, biasT.b], start=True, stop=False)
                                P.mm(bk.b, bk.t[:, h4 * 128:(h4 + 1) * 128], kbuf.t[p0:p0 + 64, kcol:kcol + 128],
                                     qT.t[p0:p0 + 64, h // 2, j * 128:(j + 1) * 128], [kbuf.b, qT.b],
                                     start=False, stop=True)
                            P.act(pT[kb].b, pT[kb].t[:, half * 512:(half + 1) * 512], bk.t[:, :], AF.Exp, [bk.b])
                    if dbg == "cp4a":
                        return finish()
                    for half in range(2):
                        bo = next_pf()
                        bd = next_pf()
                        for i, kb in enumerate(kbs):
                            P.mm(bo.b, bo.t[0:64, :], vtok.t[:, j + kb, :], pT[kb].t[:, half * 512:(half + 1) * 512],
                                 [vtok.b, pT[kb].b], start=(i == 0), stop=(i == len(kbs) - 1))
                        for i, kb in enumerate(kbs):
                            P.mm(bd.b, bd.t[0:64, :], ones_b.t[:, 0:64], pT[kb].t[:, half * 512:(half + 1) * 512],
                                 [ones_b.b, pT[kb].b], start=(i == 0), stop=(i == len(kbs) - 1))
                        P.tt(den.b, den.t[0:64, :], bd.t[0:64, :], sinkexp.t[:, half * 512:(half + 1) * 512], ALU.add,
                             [bd.b, sinkexp.b])
                        P.recip(den.b, den.t[0:64, :], den.t[0:64, :], [den.b])
                        oaT = oaTs[j % 2]
                        P.tt(oaT.b, oaT.t[:, half * 4:(half + 1) * 4, :],
                             bo.t[0:64, :].rearrange("p (h q) -> p h q", h=4),
                             den.t[0:64, :].rearrange("p (h q) -> p h q", h=4), ALU.mult, [bo.b, den.b])
                    oaT = oaTs[j % 2]
                    if dbg == "cp4b":
                        return finish()
                    ci, dst = oc_dst(0, 512, tok0 + j * 128)
                    c0_ = (tok0 + j * 128) % CH
                    P.dma("sp", dst[:, c0_:c0_ + 128].rearrange("(h d) t -> d h t", h=8),
                          oaT.t[:, :, :], oaT.slot, [oaT.b], [ocat_b[ci]])
                    if dbg == "cp4c":
                        return finish()
                P.copy("dve", kbuf.b, kbuf.t[:, 0:128], kbuf.t[:, 512:640], [kbuf.b])
                P.copy("dve", vtok.b, vtok.t[:, 0, :], vtok.t[:, 4, :], [vtok.b])

                if dbg == "cp5":
                    return finish()
                for mg in range(2):
                    pf = next_pf()
                    P.mm(pf.b, pf.t[:, :], mkT.t[:, mg * 128:(mg + 1) * 128], mqT.t[:, :], [mkT.b, mqT.b])
                    P.act(pmT.b, pmT.t[:, mg, :], pf.t[:, :], AF.Exp, [pf.b])
                bo = next_pf()
                bd = next_pf()
                for mg in range(2):
                    P.mm(bo.b, bo.t[:, :], mvt.t[:, mg, :], pmT.t[:, mg, :], [mvt.b, pmT.b], start=(mg == 0), stop=(mg == 1))
                for mg in range(2):
                    P.mm(bd.b, bd.t[:, :], ones_b.t[:, :], pmT.t[:, mg, :], [ones_b.b, pmT.b], start=(mg == 0), stop=(mg == 1))
                P.recip(den.b, den.t[:, :], bd.t[:, :], [bd.b])
                P.tt(omT.b, omT.t[:, :], bo.t[:, :], den.t[:, :], ALU.mult, [bo.b, den.b])
                for part in range(512 // CH):
                    ci, dst = oc_dst(1024, 128, tok0 + part * CH)
                    P.dma("sp", dst, omT.t[:, part * CH:(part + 1) * CH], omT.slot, [omT.b], [ocat_b[ci]])

                if dbg == "cp6":
                    return finish()
                for pair in range(4):
                    chains = [(pair, hd) for hd in range(4)]
                    cx = {ch: ctxs[i] for i, ch in enumerate(chains)}
                    col = {ch: ch[0] * 4 + ch[1] for ch in chains}
                    tsl = {ch: slice(ch[0] * 128, (ch[0] + 1) * 128) for ch in chains}
                    pA = {}
                    for ch in chains:
                        c = cx[ch]
                        cc = col[ch]
                        P.ts(c.gB.b, c.gB.t[:, :], ones_f.t[:, :], g16[:, cc:cc + 1], ALU.mult, [ones_f.b, g_t.b])
                    for ch in chains:
                        c = cx[ch]
                        pA[ch] = next_pf()
                        P.mm(pA[ch].b, pA[ch].t[:, 0:128], c.gB.t[:, :], U_f.t[:, :], [c.gB.b, U_f.b])
                    for ch in chains:
                        c = cx[ch]
                        cc = col[ch]
                        P.tt(c.X.b, c.X.t[:, :], maskneg.t[:, :], pA[ch].t[:, 0:128], ALU.subtract, [maskneg.b, pA[ch].b])
                        P.act(c.Egc.b, c.Egc.t[:, :], pA[ch].t[:, 0:128], AF.Exp, [pA[ch].b])
                        P.act(c.Dm.b, c.Dm.t[:, :], c.X.t[:, :], AF.Exp, [c.X.b, gcg.b], bias=gcg.t[:, cc:cc + 1])
                        P.tt(c.Dst.b, c.Dst.t[:, :], c.Dm.t[:, :], mstrict.t[:, :], ALU.mult, [c.Dm.b, mstrict.b])
                        P.tt(c.qt.b, c.qt.t[:, :], qh[ch[1]].t[:, tsl[ch]], c.Egc.t[:, :], ALU.mult, [qh[ch[1]].b, c.Egc.b])
                    if dbg == "g1":
                        return finish()
                    pB = {}
                    for ch in chains:
                        hd = ch[1]
                        pB[ch] = next_pf()
                        P.mm(pB[ch].b, pB[ch].t[:, 0:128], kh[hd].t[:, tsl[ch]], kh[hd].t[:, tsl[ch]], [kh[hd].b])
                        P.mm(pB[ch].b, pB[ch].t[:, 128:256], qh[hd].t[:, tsl[ch]], kh[hd].t[:, tsl[ch]], [kh[hd].b, qh[hd].b])
                    for ch in chains:
                        c = cx[ch]
                        cc = col[ch]
                        bcol = beta_t.t[:, :, :].rearrange("p j c -> p (j c)")[:, cc:cc + 1]
                        P.stt(c.L[0].b, c.L[0].t[:, 0:128], pB[ch].t[:, 0:128], bcol, c.Dst.t[:, :], ALU.mult, ALU.mult,
                              [pB[ch].b, beta_t.b, c.Dst.b])
                        P.tt(c.att.b, c.att.t[:, :], pB[ch].t[:, 128:256], c.Dm.t[:, :], ALU.mult, [pB[ch].b, c.Dm.b])
                    if dbg == "g2":
                        return finish()
                    pTr = {}
                    for ch in chains:
                        c = cx[ch]
                        hd = ch[1]
                        pTr[ch] = next_pb()
                        pb = pTr[ch]
                        pfa = next_pf()
                        P.tr(pfa.b, pfa.t[:, 0:128], c.L[0].t[:, 0:128], ident_f.t[:, :], [c.L[0].b, ident_f.b])
                        P.tr(pb.b, pb.t[:, 1, :], c.att.t[:, :], ident_b.t[:, :], [c.att.b, ident_b.b])
                        P.tr(pb.b, pb.t[:, 2, :], kh[hd].t[:, tsl[ch]], ident_b.t[:, :], [kh[hd].b, ident_b.b])
                        P.tr(pb.b, pb.t[:, 3, :], vg[hd].t[:, tsl[ch]], ident_b.t[:, :], [vg[hd].b, ident_b.b])
                        cc = col[ch]
                        bcol = beta_t.t[:, :, :].rearrange("p j c -> p (j c)")[:, cc:cc + 1]
                        P.copy("act", c.L[0].b, c.L[0].t[:, 128:256], pfa.t[:, 0:128], [pfa.b])
                        P.copy("act", c.attT.b, c.attT.t[:, :], pb.t[:, 1, :], [pb.b])
                        P.act(c.r[0].b, c.r[0].t[:, 0:128], pb.t[:, 3, :], AF.Copy, [pb.b, beta_t.b], scale=bcol)
                        P.act(c.r[0].b, c.r[0].t[:, 128:256], pb.t[:, 2, :], AF.Copy, [pb.b, bege.b], scale=bege.t[:, cc:cc + 1])
                        P.ts(c.k2.b, c.k2.t[:, :], pb.t[:, 2, :], ek2.t[:, cc:cc + 1], ALU.mult, [pb.b, ek2.b])
                    if dbg == "g3":
                        return finish()
                    cur = 0
                    for lv in range(7):
                        if lv > 0:
                            pL = {}
                            for ch in chains:
                                c = cx[ch]
                                Lp = c.L[(lv - 1) % 2]
                                pL[ch] = next_pf()
                                P.mm(pL[ch].b, pL[ch].t[:, 0:128], Lp.t[:, 128:256], Lp.t[:, 0:128], [Lp.b])
                                P.mm(pL[ch].b, pL[ch].t[:, 128:256], Lp.t[:, 0:128], Lp.t[:, 128:256], [Lp.b])
                            for ch in chains:
                                c = cx[ch]
                                Ln_ = c.L[lv % 2]
                                if lv < 6:
                                    P.copy("act", Ln_.b, Ln_.t[:, :], pL[ch].t[:, 0:256], [pL[ch].b])
                                else:
                                    P.copy("act", Ln_.b, Ln_.t[:, 128:256], pL[ch].t[:, 128:256], [pL[ch].b])
                        pR = {}
                        for ch in chains:
                            c = cx[ch]
                            Lc = c.L[lv % 2]
                            pR[ch] = next_pf()
                            P.mm(pR[ch].b, pR[ch].t[:, 0:256], Lc.t[:, 128:256], c.r[cur].t[:, :], [Lc.b, c.r[cur].b])
                        for i, ch in enumerate(chains):
                            c = cx[ch]
                            P.tt(c.r[1 - cur].b, c.r[1 - cur].t[:, :], c.r[cur].t[:, :], pR[ch].t[:, 0:256],
                                 ALU.subtract if lv == 0 else ALU.add, [c.r[cur].b, pR[ch].b])
                        cur = 1 - cur
                    if dbg == "g4":
                        return finish()
                    for ch in chains:
                        c = cx[ch]
                        pfw = next_pf()
                        P.tr(pfw.b, pfw.t[:, 0:128], c.r[cur].t[:, 128:256], ident_f.t[:, :], [c.r[cur].b, ident_f.b])
                        P.copy("act", c.wT.b, c.wT.t[:, :], pfw.t[:, 0:128], [pfw.b])
                    if dbg == "g5":
                        return finish()
                    for jj in range(1):
                        j = pair
                        chs = [(j, hd) for hd in range(4)]
                        pW = {}
                        for ch in chs:
                            c = cx[ch]
                            hd = ch[1]
                            pW[ch] = next_pf()
                            P.mm(pW[ch].b, pW[ch].t[:, 0:128], c.wT.t[:, :], Sb[hd].t[:, :], [c.wT.b, Sb[hd].b])
                        for ch in chs:
                            c = cx[ch]
                            P.tt(c.vn.b, c.vn.t[:, :], c.r[cur].t[:, 0:128], pW[ch].t[:, 0:128], ALU.subtract, [c.r[cur].b, pW[ch].b])
                        pO = {}
                        for ch in chs:
                            c = cx[ch]
                            hd = ch[1]
                            pO[ch] = next_pf()
                            P.mm(pO[ch].b, pO[ch].t[:, 0:128], Sb[hd].t[:, :], c.qt.t[:, :], [Sb[hd].b, c.qt.b], start=True, stop=False)
                            P.mm(pO[ch].b, pO[ch].t[:, 0:128], c.vn.t[:, :], c.attT.t[:, :], [c.vn.b, c.attT.b], start=False, stop=True)
                            P.mm(pO[ch].b, pO[ch].t[:, 128:256], c.k2.t[:, :], c.vn.t[:, :], [c.k2.b, c.vn.b])
                        for ch in chs:
                            c = cx[ch]
                            hd = ch[1]
                            cc = col[ch]
                            P.copy("act", oraw[hd].b, oraw[hd].t[:, tsl[ch]], pO[ch].t[:, 0:128], [pO[ch].b])
                            P.stt(Sf[hd].b, Sf[hd].t[:, :], Sf[hd].t[:, :], egt.t[:, cc:cc + 1], pO[ch].t[:, 128:256],
                                  ALU.mult, ALU.add, [Sf[hd].b, egt.b, pO[ch].b])
                            P.copy("act", Sb[hd].b, Sb[hd].t[:, :], Sf[hd].t[:, :], [Sf[hd].b])

                if dbg == "g6":
                    return finish()
                for hd in range(4):
                    sq = sqb[0]
                    rn = rnb[0]
                    P.act(sq.b, sq.t[:, :], oraw[hd].t[:, :], AF.Square, [oraw[hd].b])
                    pf = next_pf()
                    P.mm(pf.b, pf.t[:, :], ones_b.t[:, :], sq.t[:, :], [ones_b.b, sq.b])
                    P.act(rn.b, rn.t[:, :], pf.t[:, :], AF.Sqrt, [pf.b], bias=EPS, scale=1.0 / 128)
                    P.recip(rn.b, rn.t[:, :], rn.t[:, :], [rn.b])
                    P.tt(rn.b, rn.t[:, :], rn.t[:, :], oraw[hd].t[:, :], ALU.mult, [rn.b, oraw[hd].b])
                    obT = obTs[hd % 2]
                    P.stt(obT.b, obT.t[:, :], rn.t[:, :], gdn.t[:, 0:1], zT[hd].t[:, :], ALU.mult, ALU.mult,
                          [rn.b, gdn.b, zT[hd].b])
                    for part in range(512 // CH):
                        ci, dst = oc_dst(512 + hd * 128, 128, tok0 + part * CH)
                        P.dma("sp", dst, obT.t[:, part * CH:(part + 1) * CH], obT.slot, [obT.b], [ocat_b[ci]])
                if t > 0:
                    for ci in range((t - 1) * (512 // CH), t * (512 // CH)):
                        gather_chunk(ci)

            if dbg == "cp7":
                return finish()
            for ci in range((NT1 - 1) * (512 // CH), NT1 * (512 // CH)):
                gather_chunk(ci)
            if dbg:
                dslot = P.slot()
                P.dma("pool", dbg_out, ogath.ap(), dslot, ogath_b, [])
                P.flush(final_waits=[(dslot.sem, dslot.count)])
                return nc
            P.flush()

        with ExitStack() as e2:
            big = P.sb(e2, [128, 68 * T2], BF16, slot=True)
            oT_t = big.t[:, 0:36 * T2].rearrange("p (c t) -> p c t", c=36)
            yT_t = big.t[:, 36 * T2:68 * T2].rearrange("p (c t) -> p c t", c=32)
            AH = 44
            actT_t = big.t[:, 0:AH * T2].rearrange("p (c t) -> p c t", c=AH)
            sg = P.sb(e2, [128, 4, T2], BF16)
            yacc = P.sb(e2, [128, 4, T2], F32)
            ytmp = P.sb(e2, [128, T2], F32)
            xst = [P.sb(e2, [128, 512], F32, slot=True) for _ in range(2)]
            ost = [P.sb(e2, [128, 512], F32, slot=True) for _ in range(2)]
            gst = [P.sb(e2, [128, 512], F32, slot=True) for _ in range(2)]
            segsb = P.sb(e2, [1, 1], I32, slot=True)
            out_b = Buf()
            sgate = P.sb(e2, [128, 4, T2], BF16)
            reg = e2.enter_context(nc.gpsimd.register("segreg"))
            P.dma("pool", segsb.t[:, :], seg, segsb.slot, [], [segsb.b])
            off_box = {}

            def ld_reg(e):
                ins = e.reg_load(reg, segsb.t[0:1, 0:1])
                off_box["v"] = e.snap(reg, min_val=0, max_val=(NCHK - OWN // CH) * 4608)
                return ins

            P.op("pool", ld_reg, [segsb.b], [])
            ogall = ogath.ap()
            wgv = w_gate.rearrange("(kc p) n -> p kc n", p=128)
            wbv = [w_br_a.rearrange("(kc p) n -> p kc n", p=128), w_br_b.rearrange("(kc p) n -> p kc n", p=128),
                   w_br_m.rearrange("(kc p) n -> p kc n", p=128)]
            wov = w_o.rearrange("(kc p) n -> p kc n", p=128)
            wfiv = w_fi.rearrange("(kc p) n -> p kc n", p=128)
            wfov = w_fo.rearrange("(kc p) n -> p kc n", p=128)
            x1v = x1scr.ap()
            xi = {"x": 0, "o": 0, "g": 0}

            def next_xst():
                i = xi["x"] % 2
                xi["x"] += 1
                return xst[i]

            def next_ost():
                i = xi["o"] % 2
                xi["o"] += 1
                return ost[i]

            br_k = [
                [(r * 4 + jj, r * 9 + jj) for r in range(4) for jj in range(4)],
                [(r * 4 + jj, r * 9 + 4 + jj) for r in range(4) for jj in range(4)],
                [(r, r * 9 + 8) for r in range(4)],
            ]

            for tt2 in range(NT2):
                r0 = tt2 * T2
                x1b = {(g, cg): Buf() for g in range(G2) for cg in range(8)}
                load_gfull(0)
                for g in range(G2):
                    norm_transpose(x_own[r0 + g * 128:r0 + (g + 1) * 128, :], [], g, hT.t, hT.b)

                for part in range(T2 // CH):
                    lc = (r0 // CH) + part

                    def ld_o(e, lc=lc, part=part):
                        own_rows = ogall[bass.ds(off_box["v"], (OWN // CH) * 4608), :]
                        src = own_rows[lc * 4608:(lc + 1) * 4608, :].rearrange("(c p) t -> p c t", p=128)
                        return e.dma_start(out=oT_t[:, :, part * CH:(part + 1) * CH], in_=src)

                    P.custom("pool", ld_o, big.slot, 16, ogath_b, [big.b])

                rhs_h2 = lambda ri: hT.t[:, ri, 0:T2]
                rhs_o = lambda ri: oT_t[:, ri, :]
                for fg in range(8):
                    for br in range(3):
                        def ep_gate(cb, bank, br=br):
                            P.act(sg.b, sg.t[:, cb, :], bank.t[:, 0:T2], AF.Sigmoid, [bank.b])

                        stream_B(wgv, br * D + fg * 512, 512, KALL, rhs_h2, [hT.b], T2, ep_gate)

                        def ep_br(cb, bank, br=br, fg=fg):
                            if br == 0:
                                P.tt(yacc.b, yacc.t[:, cb, :], bank.t[:, 0:T2], sg.t[:, cb, :], ALU.mult, [bank.b, sg.b])
                            else:
                                P.tt(ytmp.b, ytmp.t[:, :], bank.t[:, 0:T2], sg.t[:, cb, :], ALU.mult, [bank.b, sg.b])
                                if br == 1:
                                    P.tt(yacc.b, yacc.t[:, cb, :], yacc.t[:, cb, :], ytmp.t[:, :], ALU.add, [yacc.b, ytmp.b])
                                else:
                                    P.tt(big.b, yT_t[:, fg * 4 + cb, :], yacc.t[:, cb, :], ytmp.t[:, :], ALU.add, [yacc.b, ytmp.b])

                        stream_B(wbv[br], fg * 512, 512, br_k[br], rhs_o, [big.b], T2, ep_br)

                for cg in range(8):
                    def ep_o(tg, bank, cg=cg):
                        xt = next_xst()
                        ot = next_ost()
                        rows = slice(r0 + tg * 128, r0 + (tg + 1) * 128)
                        P.dma("sp", xt.t[:, :], x_own[rows, cg * 512:(cg + 1) * 512], xt.slot, [], [xt.b])
                        P.tt(ot.b, ot.t[:, :], bank.t[:, :], xt.t[:, :], ALU.add, [bank.b, xt.b])
                        P.dma("sp", x1v[rows, cg * 512:(cg + 1) * 512], ot.t[:, :], ot.slot, [ot.b], [x1b[(tg, cg)]])

                    stream_A(wov, cg * 512, KALL, lambda ri, tg: yT_t[:, ri, tg * 128:(tg + 1) * 128], [big.b], G2, ep_o)

                load_gfull(2)
                for g in range(G2):
                    norm_transpose(x1v[r0 + g * 128:r0 + (g + 1) * 128, :], [x1b[(g, cg)] for cg in range(8)], g, hT.t, hT.b)

                for half, (c0, c1) in enumerate([(0, AH), (AH, FC)]):
                    nch = c1 - c0
                    for b0 in range(0, nch, 4):
                        nb = min(4, nch - b0)
                        col0 = (c0 + b0) * 128

                        def ep_gate2(cb, bank):
                            P.act(sgate.b, sgate.t[:, cb, :], bank.t[:, 0:T2], AF.Silu, [bank.b])

                        stream_B(wfiv, col0, nb * 128, KALL, rhs_h2, [hT.b], T2, ep_gate2)

                        def ep_up(cb, bank, b0=b0):
                            P.tt(big.b, actT_t[:, b0 + cb, :], bank.t[:, 0:T2], sgate.t[:, cb, :], ALU.mult, [bank.b, sgate.b])

                        stream_B(wfiv, D_FF + col0, nb * 128, KALL, rhs_h2, [hT.b], T2, ep_up)
                    kch = [(c0 + i, i) for i in range(nch)]
                    for cg in range(8):
                        def ep_f(tg, bank, cg=cg):
                            xt = next_xst()
                            ot = next_ost()
                            rows = slice(r0 + tg * 128, r0 + (tg + 1) * 128)
                            xb = x1b[(tg, cg)]
                            P.dma("sp", xt.t[:, :], x1v[rows, cg * 512:(cg + 1) * 512], xt.slot, [xb], [xt.b])
                            P.tt(ot.b, ot.t[:, :], bank.t[:, :], xt.t[:, :], ALU.add, [bank.b, xt.b])
                            P.dma("sp", x1v[rows, cg * 512:(cg + 1) * 512], ot.t[:, :], ot.slot, [ot.b], [xb])

                        stream_A(wfov, cg * 512, kch, lambda ri, tg: actT_t[:, ri, tg * 128:(tg + 1) * 128], [big.b], G2, ep_f)

                for g in range(G2):
                    rows = slice(r0 + g * 128, r0 + (g + 1) * 128)
                    xt = norm_group(x1v[rows, :], [x1b[(g, cg)] for cg in range(8)])
                    for q4 in range(8):
                        ot = next_ost()
                        gt = gst[xi["g"] % 2]
                        xi["g"] += 1
                        P.dma("sp", gt.t[:, :], gfin_in[:, q4 * 512:(q4 + 1) * 512], gt.slot, [], [gt.b])
                        P.stt(ot.b, ot.t[:, :], xt.t[:, q4 * 512:(q4 + 1) * 512], ss.t[:, 1:2], gt.t[:, :],
                              ALU.mult, ALU.mult, [xt.b, ss.b, gt.b])
                        P.dma("sp", out[rows, q4 * 512:(q4 + 1) * 512], ot.t[:, :], ot.slot, [ot.b], [out_b])

            finals = [tok for tok in out_b.w.values()]
            P.flush(final_waits=finals)
        print("ninst", P.ninst, "sbuf left", nc.sbuf_bytes_remaining, flush=True)
    return nc


def _t5_bucket_np(n):
    max_exact = 16
    nf = np.maximum(n, 1).astype(np.float32)
    large = max_exact + (np.log(nf / max_exact) / math.log(128 / max_exact) * (32 - max_exact)).astype(np.int32)
    large = np.minimum(large, 31)
    return np.where(n < max_exact, n, large)


def _prep_inputs(inp, S):
    f32 = np.float32
    x = np.asarray(inp["x"], f32)[:, :S]
    OWN = S // 4
    w_in = np.asarray(inp["w_in"], f32)[0]
    kk = np.arange(128)[:, None]
    qq = np.arange(128)[None, :]
    rel_bias = np.asarray(inp["rel_bias"], f32)
    tabs = []
    for kb in range(2):
        dist = qq - kk if kb == 1 else qq + 128 - kk
        valid = (dist >= 0) & (dist < 128)
        bidx = _t5_bucket_np(np.maximum(dist, 0))
        tabs.append((bidx, valid))
    cst = np.zeros((6, 128, 128), f32)
    ii = np.arange(128)[:, None]
    jj = np.arange(128)[None, :]
    cst[0] = np.eye(128, dtype=f32)
    cst[1] = (ii <= jj).astype(f32)
    cst[2] = 1.0
    cst[3] = np.where(jj <= ii, 0.0, -10000.0)
    cst[4] = (jj < ii).astype(f32)
    gcol = np.stack([np.asarray(inp[k], f32).reshape(KC, 128).T for k in ("g_mix", "g_mem", "g_ffn", "g_final")], axis=1)
    gcol = np.ascontiguousarray(gcol).astype(f32)
    gfin = np.ascontiguousarray(np.broadcast_to(np.asarray(inp["g_final"], f32).reshape(1, D), (128, D))).astype(f32)
    w_gate = np.ascontiguousarray(w_in[:, GATE0:])
    shared = {
        "w_gate": w_gate,
        "w_br_a": np.asarray(inp["w_br_a"], f32)[0], "w_br_b": np.asarray(inp["w_br_b"], f32)[0],
        "w_br_m": np.asarray(inp["w_br_m"], f32)[0], "w_o": np.asarray(inp["w_o"], f32)[0],
        "w_fi": np.asarray(inp["w_ffn_in"], f32)[0], "w_fo": np.asarray(inp["w_ffn_out"], f32)[0],
        "gcol": gcol, "gfin": gfin, "cst": cst,
        "gdn": np.asarray(inp["g_dn_out"], f32).reshape(128, 1).copy(),
    }
    conv_w = np.asarray(inp["conv_w"], f32)[0]
    w_mem = np.asarray(inp["w_mem_kv"], f32)[0]
    sinks = np.asarray(inp["sinks"], f32)[0]
    a_log = np.asarray(inp["a_log"], f32)[0]
    dt_bias = np.asarray(inp["dt_bias"], f32)[0]
    offs = np.cumsum([0, 2048, 256, 256, 6144, 2048, 16, 16, 512])
    oAq, oAk, oAv, oB, oZ, oBb, oBa, oMq = offs[:8]
    maps = []
    for c in range(8):
        b, hg = c // 4, c % 4
        cols = []
        cols += list(range(oAq + hg * 512, oAq + (hg + 1) * 512))
        kc_ = list(range(oAk + hg * 64, oAk + (hg + 1) * 64))
        cols += kc_ + kc_
        cols += list(range(oMq + hg * 128, oMq + (hg + 1) * 128))
        cols += list(range(oAv + hg * 64, oAv + (hg + 1) * 64))
        cols += list(range(oBb + hg * 4, oBb + (hg + 1) * 4))
        cols += list(range(oBa + hg * 4, oBa + (hg + 1) * 4))
        npad = 1024 - len(cols)
        cols1 = np.array(cols)
        w1 = np.zeros((D, NC1), f32)
        w1[:, :len(cols)] = w_in[:, cols1]
        for kind in range(3):
            c0 = oB + kind * 2048 + hg * 512
            w1[:, 1024 + kind * 512:1024 + (kind + 1) * 512] = w_in[:, c0:c0 + 512]
        w1[:, 2560:3072] = w_in[:, oZ + hg * 512: oZ + (hg + 1) * 512]
        convw = np.zeros((128, 12, 4), f32)
        for kind in range(3):
            for hd in range(4):
                c0 = kind * 2048 + hg * 512 + hd * 128
                convw[:, kind * 4 + hd, :] = conv_w[:, c0:c0 + 128].T
        biasT = np.zeros((128, 2, 8, 128), f32)
        for kb in range(2):
            bidx, valid = tabs[kb]
            for h in range(8):
                tab = rel_bias[bidx, hg * 8 + h]
                biasT[:, kb, h, :] = np.where(valid, tab, NEGM)
        sinkrep = np.broadcast_to(np.repeat(sinks[hg * 8:(hg + 1) * 8], 128)[None, :], (64, 1024)).astype(f32)
        m = dict(shared)
        m.update({
            "x_full": np.ascontiguousarray(x[b]),
            "x_own": np.ascontiguousarray(x[b, hg * OWN:(hg + 1) * OWN]),
            "seg": np.array([[hg * (OWN // min(256, OWN)) * 4608]], np.int32),
            "mem": np.ascontiguousarray(np.asarray(inp["mem"], f32)[b]),
            "w1": w1,
            "w_mkv": np.ascontiguousarray(np.concatenate([w_mem[:, hg * 128:(hg + 1) * 128],
                                                          w_mem[:, 512 + hg * 128:512 + (hg + 1) * 128]], axis=1)),
            "biasT": biasT.reshape(128, 2048),
            "sinkrep": np.ascontiguousarray(sinkrep),
            "convw": convw.reshape(128, 48),
            "alogrep": np.ascontiguousarray(np.broadcast_to(np.tile(a_log[hg * 4:(hg + 1) * 4], 4)[None, :], (128, 16))).astype(f32),
            "dtbrep": np.ascontiguousarray(np.broadcast_to(np.tile(dt_bias[hg * 4:(hg + 1) * 4], 4)[None, :], (128, 16))).astype(f32),
        })
        maps.append(m)
    return maps


_NC_CACHE = {}


def run(inp, S):
    if S not in _NC_CACHE:
        _NC_CACHE[S] = build(S)
    nc = _NC_CACHE[S]
    maps = _prep_inputs(inp, S)
    res = run_bass_kernel_spmd(nc, maps, core_ids=list(range(8)))
    OWN = S // 4
    outp = np.zeros((2, S, D), np.float32)
    for c in range(8):
        b, hg = c // 4, c % 4
        outp[b, hg * OWN:(hg + 1) * OWN] = np.asarray(res.results[c]["out"])
    return outp


def kernel(**inputs):
    return run(inputs, 8192)
```
